# Optimizing a Trainium2 kernel written in Bass

```python
import jax, jax.numpy as jnp
from jax import lax
import numpy as np


D_MODEL = 1024
BATCH = 8
SEQ = 4096
DEPTH = 4

GRID_W = 64
CTX_LEN = 256
N_MIXERS = 2
N_ATT_LAYERS = (DEPTH + 1) // 2
N_RET_LAYERS = DEPTH // 2

ATT_HEADS = 8
ATT_KV_HEADS = 2
ATT_HEAD_DIM = D_MODEL // ATT_HEADS
ATT_GROUP = ATT_HEADS // ATT_KV_HEADS
ATT_WIDTH = ATT_HEADS * ATT_HEAD_DIM
ATT_KV_WIDTH = ATT_KV_HEADS * ATT_HEAD_DIM
ATT_IN = 2 * ATT_WIDTH + 2 * ATT_KV_WIDTH
Q_BLOCK = 128
ROPE_THETA = 10000.0

RET_HEADS = 4
RET_QK_DIM = D_MODEL // RET_HEADS
RET_V_DIM = 2 * RET_QK_DIM
RET_QK_WIDTH = RET_HEADS * RET_QK_DIM
RET_V_WIDTH = RET_HEADS * RET_V_DIM
RET_IN = 2 * RET_QK_WIDTH + 2 * RET_V_WIDTH
RET_CHUNK = 128

DEEPNORM_ALPHA = (2.0 * DEPTH) ** 0.25
DEEPNORM_BETA = (8.0 * DEPTH) ** -0.25
LN_EPS = 1e-5
QK_EPS = 1e-6
GN_EPS = 1e-5

kernel_name = 'hybrid_gqa_retention_prefix_flow_block'


def layer_norm(x, g, b):
    xf = x.astype(jnp.float32)
    mu = xf.mean(-1, keepdims=True)
    var = jnp.square(xf - mu).mean(-1, keepdims=True)
    return ((xf - mu) * lax.rsqrt(var + LN_EPS) * g.astype(jnp.float32) + b.astype(jnp.float32)).astype(x.dtype)


def rms_norm(x, g):
    xf = x.astype(jnp.float32)
    ms = jnp.square(xf).mean(-1, keepdims=True)
    return (xf * lax.rsqrt(ms + QK_EPS) * g.astype(jnp.float32)).astype(x.dtype)


def rope_1d(x, pos):
    half = x.shape[-1] // 2
    freqs = ROPE_THETA ** (-jnp.arange(half, dtype=jnp.float32) / half)
    ang = pos.astype(jnp.float32)[:, None] * freqs[None, :]
    cos = jnp.cos(ang)[:, None, :]
    sin = jnp.sin(ang)[:, None, :]
    xf = x.astype(jnp.float32)
    x1, x2 = xf[..., :half], xf[..., half:]
    return jnp.concatenate([x1 * cos - x2 * sin, x1 * sin + x2 * cos], axis=-1).astype(x.dtype)


def axial_rope(x, row, col):
    half = x.shape[-1] // 2
    return jnp.concatenate([rope_1d(x[..., :half], row), rope_1d(x[..., half:], col)], axis=-1)


def attend(q, k, v):
    s = jnp.einsum('bqkgd,bskd->bkgqs', q, k).astype(jnp.float32) * (ATT_HEAD_DIM ** -0.5)
    p = jax.nn.softmax(s, axis=-1).astype(v.dtype)
    return jnp.einsum('bkgqs,bskd->bqkgd', p, v)


def attention_mixer(h_lat, h_ctx, w_in, w_out, q_scale, k_scale, row, col, need_ctx):
    B, S, _ = h_lat.shape
    L = h_ctx.shape[1]

    def project(h):
        n = h.shape[1]
        p = h @ w_in
        q, g, k, v = jnp.split(p, [ATT_WIDTH, 2 * ATT_WIDTH, 2 * ATT_WIDTH + ATT_KV_WIDTH], axis=-1)
        q = rms_norm(q.reshape(B, n, ATT_HEADS, ATT_HEAD_DIM), q_scale)
        k = rms_norm(k.reshape(B, n, ATT_KV_HEADS, ATT_HEAD_DIM), k_scale)
        v = v.reshape(B, n, ATT_KV_HEADS, ATT_HEAD_DIM)
        return q, k, v, g

    q_l, k_l, v_l, g_l = project(h_lat)
    q_l = axial_rope(q_l, row, col)
    k_l = axial_rope(k_l, row, col)
    q_c, k_c, v_c, g_c = project(h_ctx)
    k_all = jnp.concatenate([k_l, k_c], axis=1)
    v_all = jnp.concatenate([v_l, v_c], axis=1)

    nb = S // Q_BLOCK
    qb = q_l.reshape(B, nb, Q_BLOCK, ATT_KV_HEADS, ATT_GROUP, ATT_HEAD_DIM).transpose(1, 0, 2, 3, 4, 5)
    o_l = lax.map(lambda qblk: attend(qblk, k_all, v_all), qb)
    o_l = o_l.transpose(1, 0, 2, 3, 4, 5).reshape(B, S, ATT_WIDTH)
    y_l = (o_l * jax.nn.silu(g_l)) @ w_out
    y_c = None
    if need_ctx:
        o_c = attend(q_c.reshape(B, L, ATT_KV_HEADS, ATT_GROUP, ATT_HEAD_DIM), k_c, v_c).reshape(B, L, ATT_WIDTH)
        y_c = (o_c * jax.nn.silu(g_c)) @ w_out
    return y_l, y_c


def decayed_state(k, v, log_g):
    n = k.shape[2]
    w = jnp.exp(log_g.astype(jnp.float32)[:, None] * (n - 1 - jnp.arange(n, dtype=jnp.float32))[None, :])
    return jnp.einsum('bhld,hl,bhle->bhde', k, w.astype(k.dtype), v)


def retention_chunkwise(q, k, v, log_g, state0, strict):
    B, H, N, _ = q.shape
    dv = v.shape[-1]
    C = RET_CHUNK
    nc = N // C
    idx = jnp.arange(C, dtype=jnp.float32)
    lg = log_g.astype(jnp.float32)[:, None]
    diff = idx[:, None] - idx[None, :]
    mask = (diff > 0) if strict else (diff >= 0)
    decay = jnp.where(mask, jnp.exp(lg[:, :, None] * jnp.maximum(diff, 0.0)), 0.0).astype(q.dtype)
    xi = jnp.exp(lg * (idx + 1.0)).astype(q.dtype)
    zeta = jnp.exp(lg * (C - 1.0 - idx)).astype(q.dtype)
    g_chunk = jnp.exp(lg[:, 0] * C).astype(q.dtype)

    def chunks(t):
        return t.reshape(B, H, nc, C, t.shape[-1]).transpose(2, 0, 1, 3, 4)

    def step(state, qkv):
        qc, kc, vc = qkv
        s = jnp.einsum('bhnd,bhmd->bhnm', qc, kc) * decay
        o = jnp.einsum('bhnm,bhme->bhne', s, vc) + jnp.einsum('bhnd,bhde->bhne', qc, state) * xi[..., None]
        state = state * g_chunk[:, None, None] + jnp.einsum('bhmd,bhme->bhde', kc * zeta[..., None], vc)
        return state, o

    _, o = lax.scan(step, state0.astype(q.dtype), (chunks(q), chunks(k), chunks(v)))
    return o.transpose(1, 2, 0, 3, 4).reshape(B, H, N, dv)


def retention_mixer(h_lat, h_ctx, w_in, w_out, gn_g, lg_f, lg_b, need_ctx):
    B, S, _ = h_lat.shape
    L = h_ctx.shape[1]

    def project(h, pos):
        n = h.shape[1]
        p = h @ w_in
        q, k, v, g = jnp.split(p, [RET_QK_WIDTH, 2 * RET_QK_WIDTH, 2 * RET_QK_WIDTH + RET_V_WIDTH], axis=-1)
        q = rope_1d(q.reshape(B, n, RET_HEADS, RET_QK_DIM), pos).transpose(0, 2, 1, 3)
        k = (rope_1d(k.reshape(B, n, RET_HEADS, RET_QK_DIM), pos) * (RET_QK_DIM ** -0.5)).transpose(0, 2, 1, 3)
        v = v.reshape(B, n, RET_HEADS, RET_V_DIM).transpose(0, 2, 1, 3)
        return q, k, v, g

    def flip(t):
        return jnp.flip(t, axis=2)

    def output(o, g):
        of = o.astype(jnp.float32)
        mu = of.mean(-1, keepdims=True)
        var = jnp.square(of - mu).mean(-1, keepdims=True)
        on = ((of - mu) * lax.rsqrt(var + GN_EPS) * gn_g.astype(jnp.float32).reshape(RET_HEADS, 1, RET_V_DIM)).astype(o.dtype)
        n = o.shape[2]
        on = on.transpose(0, 2, 1, 3).reshape(B, n, RET_V_WIDTH)
        return (on * jax.nn.silu(g)) @ w_out

    q_c, k_c, v_c, g_c = project(h_ctx, jnp.arange(L))
    q_l, k_l, v_l, g_l = project(h_lat, L + jnp.arange(S))
    s_f = decayed_state(k_c, v_c, lg_f)
    s_b = decayed_state(flip(k_c), flip(v_c), lg_b)
    o_l = (retention_chunkwise(q_l, k_l, v_l, lg_f, s_f, False)
           + flip(retention_chunkwise(flip(q_l), flip(k_l), flip(v_l), lg_b, s_b, True)))
    y_l = output(o_l, g_l)
    y_c = None
    if need_ctx:
        zero = jnp.zeros_like(s_f)
        o_c = (retention_chunkwise(q_c, k_c, v_c, lg_f, zero, False)
               + flip(retention_chunkwise(flip(q_c), flip(k_c), flip(v_c), lg_b, zero, True)))
        y_c = output(o_c, g_c)
    return y_l, y_c


def setup_inputs(seed: int = 0) -> dict:
    key = jax.random.key(seed)
    ks = jax.random.split(key, 20)
    f32 = jnp.float32
    nrm = lambda k, shape: jax.random.normal(k, shape, dtype=f32)
    base_decay = jnp.log(1.0 - 2.0 ** (-5.0 - jnp.arange(RET_HEADS, dtype=f32)))
    return {
        'x': nrm(ks[0], (BATCH, SEQ, D_MODEL)),
        'c': nrm(ks[1], (BATCH, D_MODEL)),
        'ctx': nrm(ks[2], (BATCH, CTX_LEN, D_MODEL)),
        'c_ctx': nrm(ks[3], (D_MODEL,)),
        'mod_w': nrm(ks[4], (DEPTH, D_MODEL, 3 * D_MODEL)) * (0.5 * D_MODEL ** -0.5),
        'mod_b': nrm(ks[5], (DEPTH, 3 * D_MODEL)) * 0.01,
        'ln_g': 1.0 + 0.02 * nrm(ks[6], (DEPTH, D_MODEL)),
        'ln_b': 0.02 * nrm(ks[7], (DEPTH, D_MODEL)),
        'attn_w_in': nrm(ks[8], (N_ATT_LAYERS, D_MODEL, ATT_IN)) * (D_MODEL ** -0.5),
        'attn_w_out': nrm(ks[9], (N_ATT_LAYERS, ATT_WIDTH, D_MODEL)) * (ATT_WIDTH ** -0.5) * DEEPNORM_BETA,
        'attn_q_scale': 1.0 + 0.02 * nrm(ks[10], (N_ATT_LAYERS, ATT_HEAD_DIM)),
        'attn_k_scale': 1.0 + 0.02 * nrm(ks[11], (N_ATT_LAYERS, ATT_HEAD_DIM)),
        'ret_w_in': nrm(ks[12], (N_RET_LAYERS, D_MODEL, RET_IN)) * (D_MODEL ** -0.5),
        'ret_w_out': nrm(ks[13], (N_RET_LAYERS, RET_V_WIDTH, D_MODEL)) * (RET_V_WIDTH ** -0.5) * DEEPNORM_BETA,
        'ret_gn_g': 1.0 + 0.02 * nrm(ks[14], (N_RET_LAYERS, RET_V_WIDTH)),
        'ret_log_decay_fwd': base_decay[None, :] * jnp.exp(0.1 * nrm(ks[15], (N_RET_LAYERS, RET_HEADS))),
        'ret_log_decay_bwd': base_decay[None, :] * jnp.exp(0.1 * nrm(ks[16], (N_RET_LAYERS, RET_HEADS))),
    }


def reference(x, c, ctx, c_ctx, mod_w, mod_b, ln_g, ln_b, attn_w_in, attn_w_out, attn_q_scale, attn_k_scale,
              ret_w_in, ret_w_out, ret_gn_g, ret_log_decay_fwd, ret_log_decay_bwd):
    S = x.shape[1]
    ROWS = S // GRID_W
    row = jnp.repeat(jnp.arange(ROWS), GRID_W)
    col = jnp.tile(jnp.arange(GRID_W), ROWS)
    sc = jax.nn.silu(c)
    scc = jax.nn.silu(c_ctx)
    for i in range(DEPTH):
        need_ctx = i < DEPTH - 1
        shift, scale, gate = jnp.split(sc @ mod_w[i] + mod_b[i], 3, axis=-1)
        shift_c, scale_c, gate_c = jnp.split(scc @ mod_w[i] + mod_b[i], 3, axis=-1)
        h_lat = x * (1.0 + scale[:, None, :]) + shift[:, None, :]
        h_ctx = ctx * (1.0 + scale_c) + shift_c
        j = i // N_MIXERS
        if i % N_MIXERS == 0:
            y_l, y_c = attention_mixer(h_lat, h_ctx, attn_w_in[j], attn_w_out[j], attn_q_scale[j], attn_k_scale[j],
                                       row, col, need_ctx)
        else:
            y_l, y_c = retention_mixer(h_lat, h_ctx, ret_w_in[j], ret_w_out[j], ret_gn_g[j],
                                       ret_log_decay_fwd[j], ret_log_decay_bwd[j], need_ctx)
        x = layer_norm(DEEPNORM_ALPHA * x + gate[:, None, :] * y_l, ln_g[i], ln_b[i])
        if need_ctx:
            ctx = layer_norm(DEEPNORM_ALPHA * ctx + gate_c * y_c, ln_g[i], ln_b[i])
    return x
```

```python
import contextlib
import os as _os
import numpy as np
import concourse.bass as bass
import concourse.mybir as mybir
from concourse.bass_utils import run_bass_kernel_spmd

F32 = mybir.dt.float32
BF16 = mybir.dt.bfloat16
AF = mybir.ActivationFunctionType
ALU = mybir.AluOpType

D = 1024
S = 4096
L = 256
NT = S + L
DEPTH = 4
ALPHA = (2.0 * DEPTH) ** 0.25
LN_EPS = 1e-5
QK_EPS = 1e-6
GN_EPS = 1e-5
ATT_IN = 2560
RET_IN = 6144
GRID_W = 64
THETA = 10000.0


class Sem:
    def __init__(self, nc, es, name):
        self.sem = es.enter_context(nc.semaphore(name))
        self.cnt = 0


class Eng:
    def __init__(self, nc, es, eng, name):
        self.e = eng
        self.s = Sem(nc, es, "s_" + name)
        self.seen = {}

    def _need(self, toks):
        need = {}
        for t in toks:
            if t is None:
                continue
            s, c = t
            if self.seen.get(s, 0) < c and need.get(s, 0) < c:
                need[s] = c
        return list(need.items())

    def op(self, fn, *a, wait=(), sig=True, **kw):
        items = self._need(wait)
        for s, c in items[:-1]:
            self.e.wait_ge(s.sem, c)
        ins = fn(*a, **kw)
        if items:
            s, c = items[-1]
            ins._wait_ge(s.sem, c)
        for s, c in items:
            self.seen[s] = c
        if sig:
            self.s.cnt += 1
            ins.then_inc(self.s.sem, 1)
            return (self.s, self.s.cnt)
        return None

    def dma(self, out, in_, dsem, wait=()):
        items = self._need(wait)
        for s, c in items:
            self.e.wait_ge(s.sem, c)
            self.seen[s] = c
        ins = self.e.dma_start(out=out, in_=in_)
        dsem.cnt += 16
        ins.then_inc(dsem.sem, 16)
        return (dsem, dsem.cnt)

    def wait_all(self, toks):
        for s, c in self._need(toks):
            self.e.wait_ge(s.sem, c)
            self.seen[s] = c


class Slot:
    def __init__(self, buf, sem=None):
        self.buf = buf
        self.sem = sem
        self.free = []
        self.ready = None


class Ring:
    def __init__(self, slots):
        self.slots = slots
        self.i = 0

    def next(self):
        s = self.slots[self.i % len(self.slots)]
        self.i += 1
        return s


class K:
    pass


def build_program(layers=(0, 1, 2, 3)):
    nc = bass.Bass("TRN2", target_bir_lowering=False)
    k = K()
    k.nc = nc
    es = contextlib.ExitStack()
    k.es = es

    def din(name, shape, dt=F32):
        return nc.dram_tensor(name, list(shape), dt, kind="ExternalInput").ap()

    def dscr(name, shape, dt):
        return nc.dram_tensor(name, list(shape), dt).ap()

    k.x_in = din("x", [S, D])
    k.c_in = din("ctx", [L, D])
    k.cvec = din("cvec", [128, 16])
    k.mod_w = din("mod_w", [DEPTH, D, 3 * D])
    k.mod_b = din("mod_b", [DEPTH, 3 * D])
    k.ln_g = din("ln_g", [DEPTH, D])
    k.ln_b = din("ln_b", [DEPTH, D])
    k.a_w_in = din("attn_w_in", [2, D, ATT_IN])
    k.a_w_out = din("attn_w_out", [2, D, D])
    k.a_qk = din("attn_qk", [128, 4])
    k.r_w_in = din("ret_w_in", [2, D, RET_IN])
    k.r_w_out = din("ret_w_out", [2, 2 * D, D])
    k.r_gn = din("ret_gn", [128, 32])
    k.r_lg = din("ret_lg", [128, 16])
    k.cst = din("cst", [128, 128 * 8])
    k.ropeA = din("ropeA", [2, 128, S])
    k.ropeR = din("ropeR", [4, 128, NT])
    k.out = nc.dram_tensor("out", [S, D], F32, kind="ExternalOutput").ap()
    k.xs = dscr("xs", [S, D], F32)
    k.cs = dscr("cs", [L, D], F32)
    k.qT = dscr("qT", [8, 128, NT], BF16)
    k.gT = dscr("gT", [8, 128, NT], BF16)
    k.ogT = dscr("ogT", [16, 128, NT], BF16)
    k.hTd = dscr("hTd", [9, 128, 8, 512], BF16)
    k.rq = dscr("rq", [3, 8, 128, NT], BF16)
    k.rk = dscr("rk", [8, 128, NT], BF16)
    k.rkz = dscr("rkz", [2, NT, 1024], BF16)
    k.rv = dscr("rv", [NT, 2048], BF16)
    k.rg = dscr("rg", [NT, 2048], BF16)
    k.rsb = dscr("rsb", [34, 128, 8, 512], BF16)

    k.PE = Eng(nc, es, nc.tensor, "pe")
    k.ACT = Eng(nc, es, nc.scalar, "act")
    k.DVE = Eng(nc, es, nc.vector, "dve")
    k.POOL = Eng(nc, es, nc.gpsimd, "pool")
    k.SP = Eng(nc, es, nc.sync, "sp")
    k.nsem = 0
    k.ret_stop = int(_os.environ.get('RET_STOP', '0'))

    def newsem():
        k.nsem += 1
        return Sem(nc, es, f"d{k.nsem}")
    k.newsem = newsem

    k.ps = [es.enter_context(nc.psum_tensor(f"ps{i}", [128, 512], F32)) for i in range(8)]

    def sb(name, shape, dt=F32):
        return es.enter_context(nc.sbuf_tensor(name, list(shape), dt))
    k.sb = sb
    k.cstt = sb("cstt", [128, 1024])
    k.ident = k.cstt[:, 0:128]
    k.rot = k.cstt[:, 128:256]
    k.onesdiv = k.cstt[:, 256:384]
    k.RP = k.cstt[:, 384:512]
    k.RN = k.cstt[:, 512:640]
    k.IDX1 = k.cstt[:, 640:768]
    k.IDXB = k.cstt[:, 768:896]
    k.misc = k.cstt[:, 896:1024]
    k.identb = sb("identb", [128, 128], BF16)
    k.onesb = sb("onesb", [128, 128], BF16)
    k.onesdivb = sb("onesdivb", [128, 128], BF16)
    k.onesf = sb("onesf", [128, 128])
    k.rotb = sb("rotb", [128, 128], BF16)
    k.qk = sb("qk", [128, 4])
    k.gn = sb("gn", [128, 32])
    k.lg = sb("lg", [128, 16])
    k.modT = sb("modT", [128, DEPTH, 48])
    k.m2g = sb("m2g", [2, DEPTH * D])
    k.sel0 = sb("sel0", [2, 128])
    k.sel1 = sb("sel1", [2, 128])

    s0 = newsem()
    t_c = k.SP.dma(k.cstt[:], k.cst[:, :], s0)
    t_c = k.SP.dma(k.qk[:], k.a_qk[:, :], s0)
    t_c = k.SP.dma(k.gn[:], k.r_gn[:, :], s0)
    t_c = k.SP.dma(k.lg[:], k.r_lg[:, :], s0)
    k.t_const = t_c
    t1 = k.DVE.op(nc.vector.tensor_copy, out=k.identb[:], in_=k.ident, wait=[t_c])
    t2 = k.DVE.op(nc.vector.memset, k.onesb[:], 1.0)
    t2 = k.DVE.op(nc.vector.memset, k.onesf[:], 1.0)
    t3 = k.DVE.op(nc.vector.tensor_copy, out=k.onesdivb[:], in_=k.onesdiv, wait=[t_c])
    t3 = k.DVE.op(nc.vector.tensor_copy, out=k.rotb[:], in_=k.rot, wait=[t_c])
    t4 = k.DVE.op(nc.vector.memset, k.sel0[:], 0.0)
    t5 = k.DVE.op(nc.vector.memset, k.sel0[0:1, :], 1.0, wait=[t4])
    t6 = k.DVE.op(nc.vector.memset, k.sel1[:], 1.0)
    t7 = k.DVE.op(nc.vector.memset, k.sel1[0:1, :], 0.0, wait=[t6])
    k.t_init = [t_c, t1, t2, t3, t5, t7]
    k.dram_ready = {}

    k.engines = [k.PE, k.ACT, k.DVE, k.POOL, k.SP]
    k.allsems = [s0]
    k.free_sems = []
    k.borrowed = []

    def borrow():
        if k.free_sems:
            sm = k.free_sems.pop()
        else:
            sm = newsem()
            k.allsems.append(sm)
        k.borrowed.append(sm)
        return sm
    k.borrow = borrow
    k.free_sw = []
    k.borrowed_sw = []

    def borrow_sw():
        if k.free_sw:
            sm = k.free_sw.pop()
        else:
            sm = newsem()
            k.allsems.append(sm)
        k.borrowed_sw.append(sm)
        return sm
    k.borrow_sw = borrow_sw

    emit_modulation(k)
    barrier(k)
    x_src, c_src = k.x_in, k.c_in
    for i in layers:
        last = (i == DEPTH - 1)
        x_dst = k.out if last else k.xs
        need_ctx = not last
        if i % 2 == 0:
            attn_layer(k, i, x_src, c_src)
            outproj_ln(k, i, x_src, c_src, x_dst, k.cs, 8, k.a_w_out[i // 2], None, need_ctx)
        else:
            ret_layer(k, i, x_src, c_src, need_ctx)
            if k.ret_stop:
                continue
            outproj_ln(k, i, x_src, c_src, x_dst, k.cs, 16, k.r_w_out[i // 2], i // 2, need_ctx)
        x_src, c_src = x_dst, k.cs
    if tuple(layers) != (0, 1, 2, 3):
        dbg_c = nc.dram_tensor("dbg_c", [L, D], F32, kind="ExternalOutput").ap()
        sd_ = k.borrow()
        if x_src is not k.out:
            k.SP.dma(k.out[:, :], x_src[:, :], sd_)
        k.SP.dma(dbg_c[:, :], k.cs[:, :], sd_)
        barrier(k)
    es.close()
    return nc


def barrier(k):
    toks = [(E.s, E.s.cnt) for E in k.engines] + [(sm, sm.cnt) for sm in k.allsems]
    toks = [t for t in toks if t[1] > 0]
    for E in k.engines:
        E.wait_all(toks)
    k.free_sems.extend(k.borrowed)
    k.borrowed = []
    k.free_sw.extend(k.borrowed_sw)
    k.borrowed_sw = []


class Phase:
    def __init__(self, k, tag):
        self.k = k
        self.tag = tag
        self.es = contextlib.ExitStack()
        self.n = 0

    def __enter__(self):
        self.es.__enter__()
        return self

    def __exit__(self, *a):
        return self.es.__exit__(*a)

    def sb(self, shape, dt=F32, name=None):
        self.n += 1
        return self.es.enter_context(self.k.nc.sbuf_tensor(f"{self.tag}_{name or 't'}{self.n}", list(shape), dt))

    def ring(self, n, shape, dt=F32, dma=False, name=None):
        mk = (self.k.borrow_sw if dma == "sw" else self.k.borrow) if dma else None
        return Ring([Slot(self.sb(shape, dt, name), mk() if mk else None) for _ in range(n)])


def emit_modulation(k):
    nc = k.nc
    PE, ACT, DVE, POOL, SP = k.engines
    with Phase(k, "mod") as ph:
        cv = ph.sb([128, 16])
        cs_ = ph.sb([128, 16])
        mb = ph.sb([2, DEPTH * 3 * D])
        k.m2 = ph.sb([2, DEPTH * 3 * D])
        wring = ph.ring(4, [128, 8, 512], F32, dma=True)
        s1 = k.borrow()
        SP.dma(cv[:], k.cvec[:, :], s1)
        mbflat = k.mod_b.rearrange("(o i) n -> o (i n)", o=1)
        SP.dma(mb[0:1, :], mbflat, s1)
        t_ld = SP.dma(mb[1:2, :], mbflat, s1)
        t_silu = ACT.op(nc.scalar.activation, out=cs_[:], in_=cv[:], func=AF.Silu, wait=[t_ld])
        t_mb = None
        for i in range(DEPTH):
            o = i * 3 * D + D
            t_mb = DVE.op(nc.vector.tensor_scalar, out=mb[0:2, o:o + D], in0=mb[0:2, o:o + D], scalar1=1.0, scalar2=None,
                          op0=ALU.add, wait=[t_ld])
        bank_free = [[], []]
        cnt = 0
        for i in range(DEPTH):
            t_ev = None
            for n in range(6):
                slot = wring.next()
                t_w = (SP if cnt % 2 == 0 else ACT).dma(slot.buf[:], k.mod_w[i][:, n * 512:(n + 1) * 512].rearrange("(kk p) n -> p kk n", p=128),
                                                        slot.sem, wait=slot.free)
                b = cnt % 2
                cnt += 1
                t_mm = None
                for kk in range(8):
                    t_mm = PE.op(nc.tensor.matmul, k.ps[b][0:2, 0:512], lhsT=cs_[:, 2 * kk:2 * kk + 2], rhs=slot.buf[:, kk, :],
                                 start=(kk == 0), stop=(kk == 7), wait=[t_w, t_silu] + bank_free[b], sig=(kk == 7))
                slot.free = [t_mm]
                off = i * 3 * D + n * 512
                t_ev = DVE.op(nc.vector.tensor_tensor, out=k.m2[0:2, off:off + 512], in0=k.ps[b][0:2, 0:512],
                              in1=mb[0:2, off:off + 512], op=ALU.add, wait=[t_mm, t_mb])
                bank_free[b] = [t_ev]
            t_c = None
            for j in range(24):
                o = i * 3 * D + j * 128
                t_c = PE.op(nc.tensor.matmul, k.ps[2][:, 2 * j:2 * j + 2], lhsT=k.m2[0:2, o:o + 128], rhs=k.ident[0:2, 0:2],
                            start=True, stop=True, wait=[t_ev] + k.t_init + bank_free[0] * 0, sig=(j == 23))
            t_cp = DVE.op(nc.vector.tensor_copy, out=k.modT[:, i, :], in_=k.ps[2][:, 0:48], wait=[t_c])
            DVE.op(nc.vector.tensor_copy, out=k.m2g[0:2, i * D:(i + 1) * D], in_=k.m2[0:2, i * 3 * D + 2 * D:(i + 1) * 3 * D], wait=[t_ev])
            PE.wait_all([t_cp])


def attn_layer(k, i, x_src, c_src):
    nc = k.nc
    PE, ACT, DVE, POOL, SP = k.engines
    j = i // 2
    with Phase(k, f"al{i}") as lay:
        kT = lay.sb([128, 2, NT], BF16, "kT")
        V = lay.sb([128, NT // 128, 256], BF16, "V")
        with Phase(k, f"ai{i}") as ph:
            wbf = ph.sb([128, 8, ATT_IN], BF16, "wbf")
            wsem = k.borrow_sw()
            t_w = None
            for kk in range(8):
                t_w = POOL.dma(wbf[:, kk, :], k.a_w_in[j][kk * 128:(kk + 1) * 128, :], wsem)
            xs_ring = ph.ring(2, [128, 4, D], F32, dma=True, name="xs")
            hT_ring = ph.ring(2, [128, 8, 512], BF16, name="hT")
            rp_ring = ph.ring(2, [128, 2, 512], F32, dma=True, name="rope")
            sq_ring = ph.ring(2, [128, 512], BF16, name="sq")
            sd_ring = ph.ring(2, [128, 512], F32, name="sd")
            rs_ring = ph.ring(2, [128, 512], F32, name="rstd")
            xg_ring = ph.ring(3, [128, 512], BF16, name="xg")
            m1_ring = ph.ring(2, [128, 512], F32, name="m1")
            m2_ring = ph.ring(2, [128, 512], F32, name="m2")
            qst_ring = ph.ring(3, [128, 512], BF16, dma=True, name="qst")
            gst_ring = ph.ring(3, [128, 512], BF16, dma=True, name="gst")
            blocks = [(x_src[tb * 512:(tb + 1) * 512, :], tb * 512, 512, 0) for tb in range(8)] + [(c_src[0:L, :], S, L, 1)]
            loads = {}

            def issue_load(b):
                if b >= len(blocks):
                    return
                src, off, nt, r = blocks[b]
                slot = xs_ring.next()
                t = SP.dma(slot.buf[:, 0:nt // 128, :], src.rearrange("(t p) d -> p t d", p=128), slot.sem, wait=slot.free)
                rp = None
                t_r = None
                if r == 0:
                    rp = rp_ring.next()
                    SP.dma(rp.buf[:, 0, :], k.ropeA[0][:, off:off + 512], rp.sem, wait=rp.free)
                    t_r = SP.dma(rp.buf[:, 1, :], k.ropeA[1][:, off:off + 512], rp.sem)
                loads[b] = (slot, t, rp, t_r)

            issue_load(0)
            Tfree = [[], []]
            Afree = [[], [], []]
            Bfree = []
            Cfree = []
            na = 0
            eps_t = k.misc[:, 4:5]
            for b, (src, off, nt, r) in enumerate(blocks):
                issue_load(b + 1)
                slot, t_x, rp, t_r = loads.pop(b)
                hs = hT_ring.next()
                nt4 = nt // 128
                t_ev = None
                tT = None
                for kk in range(8):
                    T = k.ps[kk % 2]
                    for tt in range(nt4):
                        tT = PE.op(nc.tensor.transpose, out=T[:, tt * 128:(tt + 1) * 128], in_=slot.buf[:, tt, kk * 128:(kk + 1) * 128],
                                   identity=k.ident, wait=[t_x] + Tfree[kk % 2] + k.t_init, sig=(tt == nt4 - 1))
                    t_ev = ACT.op(nc.scalar.activation, out=hs.buf[:, kk, 0:nt], in_=T[:, 0:nt], func=AF.Identity,
                                  scale=k.modT[:, i, (8 + kk) * 2 + r:(8 + kk) * 2 + r + 1],
                                  bias=k.modT[:, i, kk * 2 + r:kk * 2 + r + 1], wait=[tT] + hs.free)
                    Tfree[kk % 2] = [t_ev]
                slot.free = [tT]
                pend = []

                def proj_fm(oc):
                    nonlocal na
                    A = k.ps[2 + na % 3]
                    ai = na % 3
                    na += 1
                    tA = None
                    for kk in range(8):
                        tA = PE.op(nc.tensor.matmul, A[:, 0:nt], lhsT=wbf[:, kk, oc * 128:(oc + 1) * 128], rhs=hs.buf[:, kk, 0:nt],
                                   start=(kk == 0), stop=(kk == 7), wait=[t_ev, t_w] + Afree[ai], sig=(kk == 7))
                    return A, ai, tA

                def qk_chunk(oc, is_k, hidx):
                    nonlocal Bfree, Cfree
                    A, ai, tA = proj_fm(oc)
                    gcol = k.qk[:, 2 * j + (1 if is_k else 0):2 * j + (1 if is_k else 0) + 1]
                    sq = sq_ring.next()
                    t1 = ACT.op(nc.scalar.activation, out=sq.buf[:, 0:nt], in_=A[:, 0:nt], func=AF.Square, wait=[tA] + sq.free)
                    st = {}

                    def stage2():
                        nonlocal Bfree
                        Bk = k.ps[5]
                        t2 = PE.op(nc.tensor.matmul, Bk[:, 0:nt], lhsT=k.onesdivb[:], rhs=sq.buf[:, 0:nt], start=True, stop=True,
                                   wait=[t1] + Bfree + k.t_init)
                        sq.free = [t2]
                        sd = sd_ring.next()
                        t3 = ACT.op(nc.scalar.activation, out=sd.buf[:, 0:nt], in_=Bk[:, 0:nt], func=AF.Ln, bias=eps_t,
                                    wait=[t2] + sd.free)
                        Bfree = [t3]
                        rs = rs_ring.next()
                        t4 = ACT.op(nc.scalar.activation, out=rs.buf[:, 0:nt], in_=sd.buf[:, 0:nt], func=AF.Exp, scale=-0.5,
                                    wait=[t3] + rs.free)
                        sd.free = [t4]
                        xg = xg_ring.next()
                        t5 = DVE.op(nc.vector.scalar_tensor_tensor, out=xg.buf[:, 0:nt], in0=A[:, 0:nt], scalar=gcol,
                                    in1=rs.buf[:, 0:nt], op0=ALU.mult, op1=ALU.mult, wait=[t4, tA] + xg.free)
                        rs.free = [t5]
                        Afree[ai] = [t1, t5]
                        st["xg"] = xg
                        st["t5"] = t5

                    def stage3():
                        nonlocal Cfree
                        xg = st["xg"]
                        t5 = st["t5"]
                        if is_k:
                            dest = kT[:, hidx, off:off + nt]
                            qs_ = None
                        else:
                            qs_ = qst_ring.next()
                            dest = qs_.buf[:, 0:nt]
                        if r == 0:
                            Ck = k.ps[6]
                            t6 = PE.op(nc.tensor.matmul, Ck[:, 0:nt], lhsT=k.rotb[:], rhs=xg.buf[:, 0:nt], start=True, stop=True,
                                       wait=[t5] + Cfree + k.t_init)
                            m1 = m1_ring.next()
                            t7 = POOL.op(nc.gpsimd.tensor_tensor, out=m1.buf[:, 0:nt], in0=xg.buf[:, 0:nt], in1=rp.buf[:, 0, 0:nt],
                                         op=ALU.mult, wait=[t5, t_r] + m1.free)
                            xg.free = [t6, t7]
                            m2 = m2_ring.next()
                            t8 = DVE.op(nc.vector.tensor_tensor, out=m2.buf[:, 0:nt], in0=Ck[:, 0:nt], in1=rp.buf[:, 1, 0:nt],
                                        op=ALU.mult, wait=[t6, t_r] + m2.free)
                            Cfree = [t8]
                            t9 = POOL.op(nc.gpsimd.tensor_tensor, out=dest, in0=m1.buf[:, 0:nt], in1=m2.buf[:, 0:nt], op=ALU.add,
                                         wait=[t7, t8] + (qs_.free if qs_ else []))
                            m1.free = [t9]
                            m2.free = [t9]
                            st["rp_done"] = t9
                        else:
                            t9 = POOL.op(nc.gpsimd.tensor_copy, out=dest, in_=xg.buf[:, 0:nt], wait=[t5] + (qs_.free if qs_ else []))
                            xg.free = [t9]
                        if qs_ is not None:
                            t10 = SP.dma(k.qT[hidx][:, off:off + nt], qs_.buf[:, 0:nt], qs_.sem, wait=[t9])
                            qs_.free = [t10]
                        st["t9"] = t9
                    pend.append([stage2, stage3])

                def advance():
                    if len(pend) >= 2 and pend[-2] and len(pend[-2]) == 2:
                        pend[-2].pop(0)()
                    if len(pend) >= 3 and pend[-3] and len(pend[-3]) == 1:
                        pend[-3].pop(0)()

                def flush():
                    for p in pend:
                        while p:
                            p.pop(0)()

                for kv in range(2):
                    qk_chunk(16 + kv, True, kv)
                    advance()
                for tt in range(nt4):
                    A = k.ps[2 + na % 3]
                    ai = na % 3
                    na += 1
                    tA = None
                    for kk in range(8):
                        tA = PE.op(nc.tensor.matmul, A[:, 0:256], lhsT=hs.buf[:, kk, tt * 128:(tt + 1) * 128], rhs=wbf[:, kk, 2304:2560],
                                   start=(kk == 0), stop=(kk == 7), wait=[t_ev, t_w] + Afree[ai], sig=(kk == 7))
                    tv = DVE.op(nc.vector.tensor_copy, out=V[:, off // 128 + tt, :], in_=A[:, 0:256], wait=[tA])
                    Afree[ai] = [tv]
                    pend.append([])
                    advance()
                for h in range(8):
                    qk_chunk(h, False, h)
                    advance()
                for h in range(8):
                    A, ai, tA = proj_fm(8 + h)
                    gs_ = gst_ring.next()
                    tg = ACT.op(nc.scalar.activation, out=gs_.buf[:, 0:nt], in_=A[:, 0:nt], func=AF.Silu, wait=[tA] + gs_.free)
                    Afree[ai] = [tg]
                    tgd = SP.dma(k.gT[h][:, off:off + nt], gs_.buf[:, 0:nt], gs_.sem, wait=[tg])
                    gs_.free = [tgd]
                    pend.append([])
                    advance()
                flush()
                hs.free = [(PE.s, PE.s.cnt)]
                if rp is not None:
                    rp.free = [(POOL.s, POOL.s.cnt), (DVE.s, DVE.s.cnt)]
        barrier(k)
        with Phase(k, f"ac{i}") as ph:
            q_ring = ph.ring(2, [128, 8, 512], BF16, dma=True, name="q")
            g_ring = ph.ring(2, [128, 8, 512], BF16, dma=True, name="g")
            pt_ring = ph.ring(8, [128, 512], BF16, name="pt")
            ri_ring = ph.ring(2, [128, 512], F32, name="ri")
            tm_ring = ph.ring(2, [128, 512], F32, name="tm")
            og_ring = ph.ring(3, [128, 512], BF16, dma=True, name="og")
            acc_ring = ph.ring(2, [128, 512], F32, name="acc")
            accs = {}
            blocks = [(tb * 512, 512, list(range(NT // 128))) for tb in range(8)] + [(S, L, [32, 33])]
            loads = {}

            def issue_load2(b):
                if b >= len(blocks):
                    return
                off, nq, kts = blocks[b]
                qs_ = q_ring.next()
                tq = SP.dma(qs_.buf[:, :, 0:nq], k.qT[:, :, off:off + nq].rearrange("h p n -> p h n"), qs_.sem, wait=qs_.free)
                gs_ = g_ring.next()
                tg = SP.dma(gs_.buf[:, :, 0:nq], k.gT[:, :, off:off + nq].rearrange("h p n -> p h n"), gs_.sem, wait=gs_.free)
                loads[b] = (qs_, tq, gs_, tg)

            Sfree = [[], [], []]
            Ofree = [[], []]
            steps = []
            grp = 0
            for b, (off, nq, kts) in enumerate(blocks):
                for h in range(8):
                    for kt in kts:
                        steps.append(dict(b=b, off=off, nq=nq, h=h, kt=kt, j=kts.index(kt), first=(kt == kts[0]), last=(kt == kts[-1]), grp=grp,
                                          blast=(h == 7 and kt == kts[-1])))
                    grp += 1
            scale = float(128 ** -0.5)
            cur = {}

            def emit_S(idx, st):
                if st["h"] == 0 and st["first"]:
                    if st["b"] == 0:
                        issue_load2(0)
                        issue_load2(1)
                    cur[st["b"]] = loads.pop(st["b"])
                qs_, tq, gs_, tg = cur[st["b"]]
                nq = st["nq"]
                Sb = k.ps[idx % 3]
                kvh = st["h"] // 4
                tS = PE.op(nc.tensor.matmul, Sb[:, 0:nq], lhsT=kT[:, kvh, st["kt"] * 128:(st["kt"] + 1) * 128], rhs=qs_.buf[:, st["h"], 0:nq],
                           start=True, stop=True, wait=[tq] + Sfree[idx % 3])
                if st["blast"]:
                    qs_.free = [tS]
                pt = pt_ring.next()
                tE = ACT.op(nc.scalar.activation, out=pt.buf[:, 0:nq], in_=Sb[:, 0:nq], func=AF.Exp, scale=scale, wait=[tS] + pt.free)
                Sfree[idx % 3] = [tE]
                st["pt"] = pt
                st["tE"] = tE

            def emit_PV(st):
                qs_, tq, gs_, tg = cur[st["b"]]
                nq = st["nq"]
                g2 = st["grp"] % 2
                Ob = k.ps[3 + g2]
                Rb = k.ps[5 + g2]
                kvh = st["h"] // 4
                pt = st["pt"]
                j_ = st["j"]
                if j_ % 2 == 0:
                    PE.op(nc.tensor.matmul, Ob[:, 0:nq], lhsT=V[:, st["kt"], kvh * 128:(kvh + 1) * 128], rhs=pt.buf[:, 0:nq],
                          start=st["first"], stop=st["last"], wait=[st["tE"]] + (Ofree[g2] if st["first"] else []), sig=False)
                    tP = PE.op(nc.tensor.matmul, Rb[:, 0:nq], lhsT=k.onesb[:], rhs=pt.buf[:, 0:nq], start=st["first"], stop=False,
                               wait=k.t_init)
                    pt.free = [tP]
                else:
                    tP = PE.op(nc.tensor.matmul, Ob[:, 0:nq], lhsT=V[:, st["kt"], kvh * 128:(kvh + 1) * 128], rhs=pt.buf[:, 0:nq],
                               start=st["first"], stop=st["last"], wait=[st["tE"]])
                    if j_ == 1:
                        accs[st["grp"]] = acc_ring.next()
                        acc = accs[st["grp"]]
                        ta = DVE.op(nc.vector.tensor_copy, out=acc.buf[:, 0:nq], in_=pt.buf[:, 0:nq], wait=[st["tE"]] + acc.free)
                    else:
                        acc = accs[st["grp"]]
                        ta = DVE.op(nc.vector.tensor_tensor, out=acc.buf[:, 0:nq], in0=acc.buf[:, 0:nq], in1=pt.buf[:, 0:nq], op=ALU.add,
                                    wait=[st["tE"]])
                    pt.free = [tP, ta]
                if st["last"]:
                    acc = accs.pop(st["grp"])
                    tP = PE.op(nc.tensor.matmul, Rb[:, 0:nq], lhsT=k.onesf[:], rhs=acc.buf[:, 0:nq], start=False, stop=True,
                               wait=[ta] + k.t_init)
                    acc.free = [tP]
                    ri = ri_ring.next()
                    t1 = DVE.op(nc.vector.reciprocal, out=ri.buf[:, 0:nq], in_=Rb[:, 0:nq], wait=[tP] + ri.free)
                    tm = tm_ring.next()
                    t2 = DVE.op(nc.vector.tensor_tensor, out=tm.buf[:, 0:nq], in0=Ob[:, 0:nq], in1=ri.buf[:, 0:nq], op=ALU.mult,
                                wait=[t1] + tm.free)
                    ri.free = [t2]
                    Ofree[g2] = [t2]
                    og = og_ring.next()
                    t3 = POOL.op(nc.gpsimd.tensor_tensor, out=og.buf[:, 0:nq], in0=tm.buf[:, 0:nq], in1=gs_.buf[:, st["h"], 0:nq],
                                 op=ALU.mult, wait=[t2, tg] + og.free)
                    tm.free = [t3]
                    t4 = SP.dma(k.ogT[st["h"]][:, st["off"]:st["off"] + nq], og.buf[:, 0:nq], og.sem, wait=[t3])
                    og.free = [t4]
                    if st["blast"]:
                        gs_.free = [t3]
                        issue_load2(st["b"] + 2)

            LAG = 2
            for idx, st in enumerate(steps):
                emit_S(idx, st)
                if idx >= LAG:
                    emit_PV(steps[idx - LAG])
            for st in steps[-LAG:]:
                emit_PV(st)
    barrier(k)


def outproj_ln(k, i, x_src, c_src, x_dst, c_dst, nW, w_out, gn_idx, need_ctx):
    nc = k.nc
    PE, ACT, DVE, POOL, SP = k.engines
    with Phase(k, f"op{i}") as ph:
        wbf = ph.sb([128, nW, D], BF16, "wbf")
        t_w = None
        if gn_idx is None:
            wsem = k.borrow_sw()
            for wc in range(nW):
                t_w = POOL.dma(wbf[:, wc, :], w_out[wc * 128:(wc + 1) * 128, :], wsem)
        else:
            stg = ph.ring(2, [128, D], F32, dma=True, name="wstg")
            for wc in range(nW):
                sl = stg.next()
                tl = SP.dma(sl.buf[:], w_out[wc * 128:(wc + 1) * 128, :], sl.sem, wait=sl.free)
                t_w = DVE.op(nc.vector.tensor_scalar, out=wbf[:, wc, :], in0=sl.buf[:], scalar1=k.gn[:, gn_idx * 16 + wc:gn_idx * 16 + wc + 1],
                             scalar2=None, op0=ALU.mult, wait=[tl] + k.t_init)
                sl.free = [t_w]
        gate_bc = ph.sb([128, 2, D], F32, "gate")
        ln_bc = ph.sb([128, 2, D], F32, "lnbc")
        lnrow = ph.sb([1, 2 * D], F32, "lnrow")
        s1 = k.borrow()
        SP.dma(lnrow[0:1, 0:D], k.ln_g[i:i + 1, :], s1)
        t_ln = SP.dma(lnrow[0:1, D:2 * D], k.ln_b[i:i + 1, :], s1)
        bfree = [[], []]
        cnt = 0
        t_bc = None
        for r in range(2):
            for nh in range(2):
                bnk = k.ps[cnt % 2]
                o = i * D + nh * 512
                t = PE.op(nc.tensor.matmul, bnk[:, :], lhsT=(k.sel0 if r == 0 else k.sel1)[:], rhs=k.m2g[0:2, o:o + 512], start=True, stop=True,
                          wait=k.t_init + bfree[cnt % 2])
                t_bc = ACT.op(nc.scalar.copy, out=gate_bc[:, r, nh * 512:(nh + 1) * 512], in_=bnk[:, :], wait=[t])
                bfree[cnt % 2] = [t_bc]
                cnt += 1
        for q in range(2):
            for nh in range(2):
                bnk = k.ps[cnt % 2]
                o = q * D + nh * 512
                t = PE.op(nc.tensor.matmul, bnk[:, :], lhsT=k.sel0[0:1, :], rhs=lnrow[0:1, o:o + 512], start=True, stop=True,
                          wait=k.t_init + [t_ln] + bfree[cnt % 2])
                t_bc = ACT.op(nc.scalar.copy, out=ln_bc[:, q, nh * 512:(nh + 1) * 512], in_=bnk[:, :], wait=[t])
                bfree[cnt % 2] = [t_bc]
                cnt += 1
        og_ring = ph.ring(2, [128, nW, 512], BF16, dma=True, name="og")
        xs_ring = ph.ring(2, [128, 4, D], F32, dma=True, name="xs")
        t_ring = ph.ring(3, [128, D], F32, name="t")
        u_ring = ph.ring(2, [128, D], F32, name="u")
        o_ring = ph.ring(3, [128, D], F32, dma=True, name="o")
        st_ring = ph.ring(3, [128, 16], F32, name="st")
        blocks = [(x_src, x_dst, tb * 512, tb * 512, 512, 0) for tb in range(8)]
        if need_ctx:
            blocks.append((c_src, c_dst, 0, S, L, 1))
        loads = {}

        def issue_load(b):
            if b >= len(blocks):
                return
            src, dst, row0, off, nt, r = blocks[b]
            ogs = og_ring.next()
            t_og = SP.dma(ogs.buf[:, :, 0:nt], k.ogT[0:nW, :, off:off + nt].rearrange("w p n -> p w n"), ogs.sem, wait=ogs.free)
            xs_ = xs_ring.next()
            t_x = SP.dma(xs_.buf[:, 0:nt // 128, :], src[row0:row0 + nt, :].rearrange("(t p) d -> p t d", p=128), xs_.sem, wait=xs_.free)
            loads[b] = (ogs, t_og, xs_, t_x)

        tails = []

        def tail_body(tb_, sts, t1, ta, row0, tt, dst):
            t2 = DVE.op(nc.vector.bn_aggr, out=sts.buf[:, 12:14], in_=sts.buf[:, 0:12], wait=[t1])
            t3 = ACT.op(nc.scalar.activation, out=sts.buf[:, 14:15], in_=sts.buf[:, 13:14], func=AF.Sqrt, bias=eps_ln, wait=[t2])
            t4 = DVE.op(nc.vector.reciprocal, out=sts.buf[:, 15:16], in_=sts.buf[:, 14:15], wait=[t3])
            t5 = DVE.op(nc.vector.tensor_scalar, out=sts.buf[:, 14:15], in0=sts.buf[:, 12:13], scalar1=sts.buf[:, 15:16], scalar2=-1.0,
                        op0=ALU.mult, op1=ALU.mult, wait=[t4])
            ub = u_ring.next()
            t6 = ACT.op(nc.scalar.activation, out=ub.buf[:], in_=tb_.buf[:], func=AF.Identity, scale=sts.buf[:, 15:16],
                        bias=sts.buf[:, 14:15], wait=[t5, ta] + ub.free)
            tb_.free = [t6]
            sts.free = [t6]
            t7 = POOL.op(nc.gpsimd.tensor_tensor, out=ub.buf[:], in0=ub.buf[:], in1=ln_bc[:, 0, :], op=ALU.mult, wait=[t6, t_bc])
            ob = o_ring.next()
            t8 = POOL.op(nc.gpsimd.tensor_tensor, out=ob.buf[:], in0=ub.buf[:], in1=ln_bc[:, 1, :], op=ALU.add, wait=[t7] + ob.free)
            ub.free = [t8]
            r0 = row0 + tt * 128
            t9 = SP.dma(dst[r0:r0 + 128, :], ob.buf[:], ob.sem, wait=[t8])
            ob.free = [t9]

        issue_load(0)
        Yfree = [[], [], [], []]
        yc = 0
        eps_ln = k.misc[:, 5:6]
        for b, (src, dst, row0, off, nt, r) in enumerate(blocks):
            issue_load(b + 1)
            ogs, t_og, xs_, t_x = loads.pop(b)
            for tt in range(nt // 128):
                tb_ = t_ring.next()
                tys = []
                for nh in range(2):
                    Y = k.ps[3 + yc % 4]
                    yi = yc % 4
                    yc += 1
                    tY = None
                    for wc in range(nW):
                        tY = PE.op(nc.tensor.matmul, Y[:, :], lhsT=ogs.buf[:, wc, tt * 128:(tt + 1) * 128], rhs=wbf[:, wc, nh * 512:(nh + 1) * 512],
                                   start=(wc == 0), stop=(wc == nW - 1), wait=[t_og, t_w] + Yfree[yi], sig=(wc == nW - 1))
                    ty = DVE.op(nc.vector.tensor_tensor, out=tb_.buf[:, nh * 512:(nh + 1) * 512], in0=Y[:, :],
                                in1=gate_bc[:, r, nh * 512:(nh + 1) * 512], op=ALU.mult, wait=[tY, t_bc] + tb_.free)
                    Yfree[yi] = [ty]
                    tys.append(ty)
                ta = DVE.op(nc.vector.scalar_tensor_tensor, out=tb_.buf[:], in0=xs_.buf[:, tt, :], scalar=float(ALPHA), in1=tb_.buf[:],
                            op0=ALU.mult, op1=ALU.add, wait=tys + [t_x])
                sts = st_ring.next()
                DVE.op(nc.vector.bn_stats, out=sts.buf[:, 0:6], in_=tb_.buf[:, 0:512], wait=[ta] + sts.free, sig=False)
                t1 = DVE.op(nc.vector.bn_stats, out=sts.buf[:, 6:12], in_=tb_.buf[:, 512:1024])
                while tails:
                    tails.pop(0)()

                def tail(tb_=tb_, sts=sts, t1=t1, ta=ta, row0=row0, tt=tt, dst=dst):
                    tail_body(tb_, sts, t1, ta, row0, tt, dst)
                tails.append(tail)
            ogs.free = [(PE.s, PE.s.cnt)]
            xs_.free = [(DVE.s, DVE.s.cnt)]
        while tails:
            tails.pop(0)()
    barrier(k)


def ret_layer(k, i, x_src, c_src, need_ctx):
    nc = k.nc
    PE, ACT, DVE, POOL, SP = k.engines
    j = i // 2
    lgc = lambda d, h: k.lg[:, j * 8 + d * 4 + h:j * 8 + d * 4 + h + 1]
    blocks = [(x_src[tb * 512:(tb + 1) * 512, :], tb * 512, 512, 0) for tb in range(8)] + [(c_src[0:L, :], S, L, 1)]
    with Phase(k, f"rl{i}") as lay:
        xi = lay.sb([128, 8, 128], F32, "xi")
        zc = lay.sb([128, 16], F32, "zc")
        DT = lay.sb([128, 4, 128], F32, "DT")
        tmpd = lay.sb([128, 128], F32, "tmpd")
        t_dec = []
        for h in range(4):
            t_dec.append(ACT.op(nc.scalar.activation, out=xi[:, h * 2, :], in_=k.IDX1, func=AF.Exp, scale=lgc(0, h), wait=k.t_init))
            t_dec.append(ACT.op(nc.scalar.activation, out=xi[:, h * 2 + 1, :], in_=k.IDXB, func=AF.Exp, scale=lgc(1, h)))
            ACT.op(nc.scalar.activation, out=zc[:, h:h + 1], in_=k.misc[:, 0:1], func=AF.Exp, scale=lgc(0, h))
            ACT.op(nc.scalar.activation, out=zc[:, 4 + h:5 + h], in_=k.misc[:, 1:2], func=AF.Exp, scale=lgc(1, h))
            ACT.op(nc.scalar.activation, out=zc[:, 8 + h:9 + h], in_=k.misc[:, 2:3], func=AF.Exp, scale=lgc(0, h))
            t_dec.append(ACT.op(nc.scalar.activation, out=zc[:, 12 + h:13 + h], in_=k.misc[:, 2:3], func=AF.Exp, scale=lgc(1, h)))
            ta = DVE.op(nc.vector.tensor_scalar, out=tmpd[:], in0=k.RP, scalar1=lgc(0, h), scalar2=None, op0=ALU.mult,
                        wait=k.t_init + t_dec[-4:])
            tb2 = DVE.op(nc.vector.scalar_tensor_tensor, out=tmpd[:], in0=k.RN, scalar=lgc(1, h), in1=tmpd[:], op0=ALU.mult, op1=ALU.add,
                         wait=[ta])
            t_dec.append(ACT.op(nc.scalar.activation, out=DT[:, h, :], in_=tmpd[:], func=AF.Exp, wait=[tb2]))
        t_dec = [t_dec[-1], t_dec[-2]]
        with Phase(k, f"ra{i}") as ph:
            wbf = ph.sb([128, 8, 2048], BF16, "wbf")
            wsem = k.borrow_sw()
            t_w = None
            for kk in range(8):
                t_w = POOL.dma(wbf[:, kk, :], k.r_w_in[j][kk * 128:(kk + 1) * 128, 0:2048], wsem)
            xs_ring = ph.ring(2, [128, 4, D], F32, dma=True, name="xs")
            hT_ring = ph.ring(2, [128, 8, 512], BF16, dma=True, name="hT")
            rp_ring = ph.ring(2, [128, 4, 512], F32, dma=True, name="rope")
            tt_ring = [ph.ring(2, [128, 512], F32, name=f"t{q}") for q in range(4)]
            o_ring = [ph.ring(2, [128, 512], F32, name=f"o{q}") for q in range(2)]
            qst_ring = ph.ring(4, [128, 3, 512], BF16, dma=True, name="qst")
            kst_ring = ph.ring(2, [128, 2, 512], BF16, dma=True, name="kst")
            kz_ring = ph.ring(2, [128, 4, 2, 256], BF16, dma=True, name="kz")
            loads = {}

            def issue_load(b):
                if b >= len(blocks):
                    return
                src, off, nt, r = blocks[b]
                slot = xs_ring.next()
                t = SP.dma(slot.buf[:, 0:nt // 128, :], src.rearrange("(t p) d -> p t d", p=128), slot.sem, wait=slot.free)
                rp = rp_ring.next()
                t_r = SP.dma(rp.buf[:, :, 0:nt], k.ropeR[:, :, off:off + nt].rearrange("c p n -> p c n"), rp.sem, wait=rp.free)
                loads[b] = (slot, t, rp, t_r)

            issue_load(0)
            Tfree = [[], []]
            Afree = [[], [], [], []]
            Pfree = []
            kz_pending = []
            na = 0
            for b, (src, off, nt, r) in enumerate(blocks):
                issue_load(b + 1)
                slot, t_x, rp, t_r = loads.pop(b)
                hs = hT_ring.next()
                nt4 = nt // 128
                t_ev = None
                tT = None
                for kk in range(8):
                    T = k.ps[kk % 2]
                    for tt in range(nt4):
                        tT = PE.op(nc.tensor.transpose, out=T[:, tt * 128:(tt + 1) * 128], in_=slot.buf[:, tt, kk * 128:(kk + 1) * 128],
                                   identity=k.ident, wait=[t_x] + Tfree[kk % 2] + k.t_init, sig=(tt == nt4 - 1))
                    t_ev = ACT.op(nc.scalar.activation, out=hs.buf[:, kk, 0:nt], in_=T[:, 0:nt], func=AF.Identity,
                                  scale=k.modT[:, i, (8 + kk) * 2 + r:(8 + kk) * 2 + r + 1],
                                  bias=k.modT[:, i, kk * 2 + r:kk * 2 + r + 1], wait=[tT] + hs.free)
                    Tfree[kk % 2] = [t_ev]
                slot.free = [tT]
                t_hst = SP.dma(k.hTd[b][:, :, 0:nt], hs.buf[:, :, 0:nt], hs.sem, wait=[t_ev])
                ra_mode = int(_os.environ.get('RA_MODE', '3'))
                for h in range(4 if ra_mode > 0 else 0):
                    for is_k in ((0, 1) if ra_mode > 1 else (0,)):
                        banks = []
                        for dkc in range(2):
                            oc = (8 if is_k else 0) + h * 2 + dkc
                            ai = na % 4
                            na += 1
                            A = k.ps[2 + ai]
                            tA = None
                            for kk in range(8):
                                tA = PE.op(nc.tensor.matmul, A[:, 0:nt], lhsT=wbf[:, kk, oc * 128:(oc + 1) * 128], rhs=hs.buf[:, kk, 0:nt],
                                           start=(kk == 0), stop=(kk == 7), wait=[t_ev, t_w] + Afree[ai], sig=(kk == 7))
                            banks.append((A, ai, tA))
                        (A, aA, tA), (B, aB, tB) = banks
                        if not is_k:
                            while kz_pending:
                                kz_pending.pop(0)()
                        co = 2 if is_k else 0
                        cosT = rp.buf[:, co, 0:nt]
                        sinT = rp.buf[:, co + 1, 0:nt]
                        ts_ = [r_.next() for r_ in tt_ring]
                        w1 = DVE.op(nc.vector.tensor_tensor, out=ts_[0].buf[:, 0:nt], in0=A[:, 0:nt], in1=cosT, op=ALU.mult, wait=[tA, t_r] + ts_[0].free)
                        w3 = DVE.op(nc.vector.tensor_tensor, out=ts_[2].buf[:, 0:nt], in0=A[:, 0:nt], in1=sinT, op=ALU.mult, wait=ts_[2].free)
                        Afree[aA] = [w3]
                        w2 = DVE.op(nc.vector.tensor_tensor, out=ts_[1].buf[:, 0:nt], in0=B[:, 0:nt], in1=sinT, op=ALU.mult, wait=[tB] + ts_[1].free)
                        w4 = DVE.op(nc.vector.tensor_tensor, out=ts_[3].buf[:, 0:nt], in0=B[:, 0:nt], in1=cosT, op=ALU.mult, wait=ts_[3].free)
                        Afree[aB] = [w4]
                        if not is_k:
                            oA = o_ring[0].next()
                            oB = o_ring[1].next()
                            pA = POOL.op(nc.gpsimd.tensor_tensor, out=oA.buf[:, 0:nt], in0=ts_[0].buf[:, 0:nt], in1=ts_[1].buf[:, 0:nt], op=ALU.subtract,
                                         wait=[w1, w2] + oA.free)
                            pB = POOL.op(nc.gpsimd.tensor_tensor, out=oB.buf[:, 0:nt], in0=ts_[2].buf[:, 0:nt], in1=ts_[3].buf[:, 0:nt], op=ALU.add,
                                         wait=[w3, w4] + oB.free)
                            for q_ in range(4):
                                ts_[q_].free = [pA, pB]
                            frees = [[], []]
                            for dkc, (osl, po) in enumerate(((oA, pA), (oB, pB))):
                                qs_ = qst_ring.next()
                                c0 = ACT.op(nc.scalar.copy, out=qs_.buf[:, 0, 0:nt], in_=osl.buf[:, 0:nt], wait=[po] + qs_.free)
                                nch = nt // 128
                                c1 = c2 = None
                                for cch in range(nch):
                                    cs0, cs1 = cch * 128, (cch + 1) * 128
                                    c1 = POOL.op(nc.gpsimd.tensor_tensor, out=qs_.buf[:, 1, cs0:cs1], in0=osl.buf[:, cs0:cs1], in1=xi[:, h * 2, :],
                                                 op=ALU.mult, wait=[po] + t_dec + qs_.free)
                                    c2 = DVE.op(nc.vector.tensor_tensor, out=qs_.buf[:, 2, cs0:cs1], in0=osl.buf[:, cs0:cs1], in1=xi[:, h * 2 + 1, :],
                                                op=ALU.mult, wait=[po] + t_dec + qs_.free)
                                osl.free = [c0, c1, c2]
                                td = SP.dma(k.rq[:, h * 2 + dkc, :, off:off + nt].rearrange("v p n -> p v n"), qs_.buf[:, :, 0:nt], qs_.sem,
                                            wait=[c0, c1, c2])
                                qs_.free = [td]
                        else:
                            ks_ = kst_ring.next()
                            oA = o_ring[0].next()
                            oB = o_ring[1].next()
                            pA = POOL.op(nc.gpsimd.tensor_tensor, out=oA.buf[:, 0:nt], in0=ts_[0].buf[:, 0:nt], in1=ts_[1].buf[:, 0:nt], op=ALU.subtract,
                                         wait=[w1, w2] + oA.free)
                            pB = POOL.op(nc.gpsimd.tensor_tensor, out=oB.buf[:, 0:nt], in0=ts_[2].buf[:, 0:nt], in1=ts_[3].buf[:, 0:nt], op=ALU.add,
                                         wait=[w3, w4] + oB.free)
                            for q_ in range(4):
                                ts_[q_].free = [pA, pB]
                            cA = ACT.op(nc.scalar.copy, out=ks_.buf[:, 0, 0:nt], in_=oA.buf[:, 0:nt], wait=[pA] + ks_.free)
                            cB = ACT.op(nc.scalar.copy, out=ks_.buf[:, 1, 0:nt], in_=oB.buf[:, 0:nt], wait=[pB])
                            td = SP.dma(k.rk[h * 2:h * 2 + 2, :, off:off + nt].rearrange("c p n -> p c n"), ks_.buf[:, :, 0:nt], ks_.sem, wait=[cA, cB])
                            ks_.free = [td]
                            if ra_mode == 2:
                                oA.free = [cA]
                                oB.free = [cB]
                                continue
                            def kz_work(h=h, oA=oA, oB=oB, cA=cA, cB=cB, nt4=nt4, off=off, nt=nt):
                                nonlocal Pfree
                                kz = kz_ring.next()
                                tp = e0 = e1 = None
                                for half in range(nt4 // 2):
                                    tts = [2 * half, 2 * half + 1]
                                    for ti, tt in enumerate(tts):
                                        for dkc, osl in enumerate((oA, oB)):
                                            tp = PE.op(nc.tensor.transpose, out=k.ps[6][:, ti * 256 + dkc * 128:ti * 256 + (dkc + 1) * 128],
                                                       in_=osl.buf[:, tt * 128:(tt + 1) * 128], identity=k.ident,
                                                       wait=[cA, cB] + Pfree + k.t_init, sig=(ti == 1 and dkc == 1))
                                    for ti, tt in enumerate(tts):
                                        e0 = ACT.op(nc.scalar.activation, out=kz.buf[:, tt, 0, :], in_=k.ps[6][:, ti * 256:(ti + 1) * 256], func=AF.Identity,
                                                    scale=zc[:, h:h + 1], wait=[tp] + t_dec + kz.free)
                                        e1 = ACT.op(nc.scalar.activation, out=kz.buf[:, tt, 1, :], in_=k.ps[6][:, ti * 256:(ti + 1) * 256], func=AF.Identity,
                                                    scale=zc[:, 4 + h:5 + h], wait=[tp] + t_dec + kz.free)
                                    Pfree = [e0, e1]
                                oA.free = [cA, e0, e1]
                                oB.free = [cB, e0, e1]
                                tz = None
                                if ra_mode == 4:
                                    kz.free = [e0, e1]
                                    return
                                for d_ in range(2):
                                    tz = SP.dma(k.rkz[d_][off:off + nt, h * 256:(h + 1) * 256].rearrange("(t p) d -> p t d", p=128),
                                                kz.buf[:, 0:nt4, d_, :], kz.sem, wait=[e0, e1])
                                kz.free = [tz]
                            kz_pending.append(kz_work)
                while kz_pending:
                    kz_pending.pop(0)()
                hs.free = [(PE.s, PE.s.cnt), t_hst]
                rp.free = [(DVE.s, DVE.s.cnt)]
        barrier(k)
        if k.ret_stop == 1:
            return
        with Phase(k, f"rb{i}") as ph:
            wbf = ph.sb([128, 8, 4096], BF16, "wbf")
            wsem = k.borrow_sw()
            t_w = None
            for kk in range(8):
                t_w = POOL.dma(wbf[:, kk, :], k.r_w_in[j][kk * 128:(kk + 1) * 128, 2048:6144], wsem)
            hT_ring = ph.ring(2, [128, 8, 512], BF16, dma=True, name="hT")
            vst_ring = ph.ring(2, [128, 2048], BF16, dma=True, name="vst")
            gst_ring = ph.ring(2, [128, 2048], BF16, dma=True, name="gst")
            SBm = ph.sb([128, 4, 2, 512], F32, "SBm")
            bl_ring = ph.ring(3, [128, 3072], BF16, dma=True, name="bl")
            sbst_ring = ph.ring(2, [128, 8, 512], BF16, dma=True, name="sbst")
            t_z2 = DVE.op(nc.vector.memset, SBm[:], 0.0)
            st_b = {h: [t_z2] for h in range(4)}
            Dfree = [[], []]
            loads = {}
            border = [8, 7, 6, 5, 4, 3, 2, 1, 0]

            def issue_loadb(bi):
                if bi >= len(border):
                    return
                b = border[bi]
                src, off, nt, r = blocks[b]
                hs = hT_ring.next()
                t = SP.dma(hs.buf[:, :, 0:nt], k.hTd[b][:, :, 0:nt], hs.sem, wait=hs.free)
                loads[bi] = (hs, t)

            tiles = []
            for bi, b in enumerate(border):
                src, off, nt, r = blocks[b]
                for tt in reversed(range(nt // 128)):
                    tiles.append((bi, b, tt, off // 128 + tt))
            vtok = {}
            bw = {}
            LAGB = 2

            def bw_load(c):
                sl = bl_ring.next()
                SP.dma(sl.buf[:, 0:1024], k.rkz[1][c * 128:(c + 1) * 128, :], sl.sem, wait=sl.free)
                t = SP.dma(sl.buf[:, 1024:3072], k.rv[c * 128:(c + 1) * 128, :], sl.sem, wait=[vtok[c]])
                bw[c] = dict(sl=sl, t_l=t)

            def bw_begin(c):
                d = bw[c]
                ss = sbst_ring.next()
                d["ss"] = ss
                d["tcs"] = [ACT.op(nc.scalar.copy, out=ss.buf[:, h * 2:h * 2 + 2, :], in_=SBm[:, h, :, :], wait=st_b[h] + ss.free)
                            for h in range(4)]

            def bw_head(c, h):
                d = bw[c]
                sl = d["sl"]
                tds = []
                for dkc in range(2):
                    Db = k.ps[6 + dkc]
                    tdm = PE.op(nc.tensor.matmul, Db[:, :], lhsT=sl.buf[:, h * 256 + dkc * 128:h * 256 + (dkc + 1) * 128],
                                rhs=sl.buf[:, 1024 + h * 512:1024 + (h + 1) * 512], start=True, stop=True, wait=[d["t_l"]] + Dfree[dkc])
                    tu = DVE.op(nc.vector.scalar_tensor_tensor, out=SBm[:, h, dkc, :], in0=SBm[:, h, dkc, :], scalar=zc[:, 12 + h:13 + h],
                                in1=Db[:, :], op0=ALU.mult, op1=ALU.add, wait=[tdm, d["tcs"][h]] + t_dec + st_b[h])
                    Dfree[dkc] = [tu]
                    tds.append(tu)
                st_b[h] = tds

            def bw_end(c):
                d = bw.pop(c)
                d["sl"].free = [(PE.s, PE.s.cnt)]
                d["ss"].free = [SP.dma(k.rsb[c][:, :, :], d["ss"].buf[:], d["ss"].sem, wait=[d["tcs"][-1]])]

            issue_loadb(0)
            Yfree = [[], [], [], []]
            yc = 0
            cur_bi = -1
            hs = t_h = None
            for ti, (bi, b, tt, c) in enumerate(tiles):
                if bi != cur_bi:
                    if hs is not None:
                        hs.free = [(PE.s, PE.s.cnt)]
                    issue_loadb(bi + 1)
                    hs, t_h = loads.pop(bi)
                    cur_bi = bi
                src, off, nt, r = blocks[b]
                cb = tiles[ti - LAGB][3] if ti >= LAGB else None
                if ti >= 1:
                    bw_load(tiles[ti - 1][3])
                if cb is not None:
                    bw_begin(cb)
                vs_ = vst_ring.next()
                gs_ = gst_ring.next()
                tv = tg = None
                for grp in range(8):
                    yi = yc % 4
                    yc += 1
                    Y = k.ps[yi]
                    tY = None
                    for kk in range(8):
                        tY = PE.op(nc.tensor.matmul, Y[:, :], lhsT=hs.buf[:, kk, tt * 128:(tt + 1) * 128], rhs=wbf[:, kk, grp * 512:(grp + 1) * 512],
                                   start=(kk == 0), stop=(kk == 7), wait=[t_h, t_w] + Yfree[yi], sig=(kk == 7))
                    if grp < 4:
                        tv = DVE.op(nc.vector.tensor_copy, out=vs_.buf[:, grp * 512:(grp + 1) * 512], in_=Y[:, :], wait=[tY] + vs_.free)
                        Yfree[yi] = [tv]
                    else:
                        g4 = grp - 4
                        tg = ACT.op(nc.scalar.activation, out=gs_.buf[:, g4 * 512:(g4 + 1) * 512], in_=Y[:, :], func=AF.Silu, wait=[tY] + gs_.free)
                        Yfree[yi] = [tg]
                    if cb is not None and grp % 2 == 1:
                        bw_head(cb, grp // 2)
                r0 = off + tt * 128
                vtok[c] = SP.dma(k.rv[r0:r0 + 128, :], vs_.buf[:], vs_.sem, wait=[tv])
                vs_.free = [vtok[c]]
                gs_.free = [SP.dma(k.rg[r0:r0 + 128, :], gs_.buf[:], gs_.sem, wait=[tg])]
                if cb is not None:
                    bw_end(cb)
            hs.free = [(PE.s, PE.s.cnt)]
            for ti in range(len(tiles), len(tiles) + LAGB):
                if ti - 1 < len(tiles):
                    bw_load(tiles[ti - 1][3])
                cb = tiles[ti - LAGB][3]
                bw_begin(cb)
                for h in range(4):
                    bw_head(cb, h)
                bw_end(cb)
        barrier(k)
        if k.ret_stop == 2:
            return
        with Phase(k, f"rc{i}") as ph:
            SFm = ph.sb([128, 4, 2, 512], F32, "SFm")
            SFb2 = ph.sb([128, 2, 4, 2, 512], BF16, "SFb")
            sfb_free = {}
            t_z1 = DVE.op(nc.vector.memset, SFm[:], 0.0)
            st_f = {h: [t_z1] for h in range(4)}
            sfb_free = {}
            Ofree = [[], [], [], []]
            Dfree = [[], []]
            stt_ = {"ST": [], "PB": []}
            eps_gn = k.misc[:, 6:7]
            r2_mode = int(_os.environ.get('R2_MODE', '3'))

            def run_seq(chunks, need_out):
                n = len(chunks)
                with Phase(k, f"rcf{i}_{chunks[0]}") as pf:
                    fq_ring = pf.ring(3, [128, 3, 8, 128], BF16, dma=True, name="fq")
                    fk_ring = pf.ring(2, [128, 8, 128], BF16, dma="sw", name="fk")
                    fl_ring = pf.ring(3, [128, 5120], BF16, dma=True, name="fl")
                    fs_ring = pf.ring(3, [128, 8, 512], BF16, dma="sw", name="fs")
                    at_ring = pf.ring(8, [128, 128], BF16, name="at")
                    on_ring = pf.ring(4, [128, 512], F32, name="on")
                    og_ring = pf.ring(8, [128, 512], F32, name="og")
                    st_ring = pf.ring(8, [128, 16], F32, name="st")
                    ot_ring = pf.ring(2, [128, 16, 256], BF16, dma=True, name="ot")
                    fl = {}
                    casts = {}
                    ats = {}
                    outs = {}

                    def load_f(idx):
                        if idx >= n:
                            return
                        c = chunks[idx]
                        tk0, tk1 = c * 128, (c + 1) * 128
                        s1 = fl_ring.next()
                        SP.dma(s1.buf[:, 0:1024], k.rkz[0][tk0:tk1, :], s1.sem, wait=s1.free)
                        SP.dma(s1.buf[:, 1024:3072], k.rv[tk0:tk1, :], s1.sem)
                        t1 = SP.dma(s1.buf[:, 3072:5120], k.rg[tk0:tk1, :], s1.sem)
                        s2 = t2 = s3 = t3 = s4 = t4 = None
                        if need_out:
                            s2 = fq_ring.next()
                            for v_ in range(3):
                                t2 = ACT.dma(s2.buf[:, v_, :, :], k.rq[v_][:, :, tk0:tk1].rearrange("c p n -> p c n"), s2.sem,
                                             wait=s2.free if v_ == 0 else [])
                            s3 = fk_ring.next()
                            t3 = POOL.dma(s3.buf[:], k.rk[:, :, tk0:tk1].rearrange("c p n -> p c n"), s3.sem, wait=s3.free)
                            s4 = fs_ring.next()
                            t4 = POOL.dma(s4.buf[:], k.rsb[c][:, :, :], s4.sem, wait=s4.free)
                        fl[idx] = (s1, t1, s2, t2, s3, t3, s4, t4)

                    def casts_A(idx):
                        par = idx % 2
                        tcps = []
                        for h in range(4):
                            tcps.append(ACT.op(nc.scalar.copy, out=SFb2[:, par, h, :, :], in_=SFm[:, h, :, :],
                                               wait=st_f[h] + sfb_free.get((par, h), [])))
                        casts[idx] = tcps

                    def dS_A(idx, h):
                        s1, t1, s2, t2, s3, t3, s4, t4 = fl[idx]
                        tcps = casts[idx]
                        Vh = s1.buf[:, 1024 + h * 512:1024 + (h + 1) * 512]
                        tds = []
                        for dkc in range(2):
                            Db = k.ps[6 + dkc]
                            tdm = PE.op(nc.tensor.matmul, Db[:, :], lhsT=s1.buf[:, h * 256 + dkc * 128:h * 256 + (dkc + 1) * 128], rhs=Vh,
                                        start=True, stop=True, wait=[t1] + Dfree[dkc])
                            tu = DVE.op(nc.vector.scalar_tensor_tensor, out=SFm[:, h, dkc, :], in0=SFm[:, h, dkc, :], scalar=zc[:, 8 + h:9 + h],
                                        in1=Db[:, :], op0=ALU.mult, op1=ALU.add, wait=[tdm, tcps[h]] + t_dec + st_f[h])
                            Dfree[dkc] = [tu]
                            tds.append(tu)
                        st_f[h] = tds
                        if not need_out and h == 3:
                            s1.free = [(PE.s, PE.s.cnt)]

                    def stage_B(idx):
                        s1, t1, s2, t2, s3, t3, s4, t4 = fl[idx]
                        tS = None
                        for h in range(4):
                            for dkc in range(2):
                                tS = PE.op(nc.tensor.matmul, k.ps[0][:, h * 128:(h + 1) * 128], lhsT=s3.buf[:, h * 2 + dkc, :],
                                           rhs=s2.buf[:, 0, h * 2 + dkc, :], start=(dkc == 0), stop=(dkc == 1), wait=[t2, t3] + stt_["ST"],
                                           sig=(h == 3 and dkc == 1))
                        s3.free = [tS]
                        lst = []
                        for h in range(4):
                            at = at_ring.next()
                            tm = DVE.op(nc.vector.tensor_tensor, out=at.buf[:], in0=k.ps[0][:, h * 128:(h + 1) * 128], in1=DT[:, h, :], op=ALU.mult,
                                        wait=[tS] + t_dec + at.free)
                            lst.append((at, tm))
                        stt_["ST"] = [lst[-1][1]]
                        ats[idx] = lst

                    tOd = {}
                    b1d = {}

                    def O_C(idx, h):
                        s1, t1, s2, t2, s3, t3, s4, t4 = fl[idx]
                        par = idx % 2
                        tcps = casts[idx]
                        at, tm = ats[idx][h]
                        Ob = k.ps[1 + h]
                        Vh = s1.buf[:, 1024 + h * 512:1024 + (h + 1) * 512]
                        PE.op(nc.tensor.matmul, Ob[:, :], lhsT=at.buf[:], rhs=Vh, start=True, stop=False, wait=[tm, t1] + Ofree[h], sig=False)
                        for dkc in range(2):
                            PE.op(nc.tensor.matmul, Ob[:, :], lhsT=s2.buf[:, 1, h * 2 + dkc, :], rhs=SFb2[:, par, h, dkc, :], start=False, stop=False,
                                  wait=[tcps[h], t2], sig=False)
                        tO = None
                        for dkc in range(2):
                            tO = PE.op(nc.tensor.matmul, Ob[:, :], lhsT=s2.buf[:, 2, h * 2 + dkc, :], rhs=s4.buf[:, h * 2 + dkc, :], start=False,
                                       stop=(dkc == 1), wait=[t4], sig=(dkc == 1))
                        at.free = [tO]
                        sfb_free[(par, h)] = [tO]
                        tOd.setdefault(idx, []).append(tO)
                        sts_ = st_ring.next()
                        b1_ = DVE.op(nc.vector.bn_stats, out=sts_.buf[:, 0:6], in_=Ob[:, :], wait=[tO] + sts_.free)
                        b1d.setdefault(idx, []).append((sts_, b1_))

                    def chain_C(idx):
                        s1, t1, s2, t2, s3, t3, s4, t4 = fl[idx]
                        tOs = tOd.pop(idx)
                        casts.pop(idx)
                        ats.pop(idx)
                        s2.free = [tOs[-1]]
                        s4.free = [tOs[-1]]
                        pre = b1d.pop(idx)
                        stsl = [p_[0] for p_ in pre]
                        b1 = [p_[1] for p_ in pre]
                        b2 = [DVE.op(nc.vector.bn_aggr, out=stsl[h].buf[:, 6:8], in_=stsl[h].buf[:, 0:6], wait=[b1[h]]) for h in range(4)]
                        b3 = [ACT.op(nc.scalar.activation, out=stsl[h].buf[:, 8:9], in_=stsl[h].buf[:, 7:8], func=AF.Sqrt, bias=eps_gn, wait=[b2[h]])
                              for h in range(4)]
                        b4 = [DVE.op(nc.vector.reciprocal, out=stsl[h].buf[:, 9:10], in_=stsl[h].buf[:, 8:9], wait=[b3[h]]) for h in range(4)]
                        b5 = [DVE.op(nc.vector.tensor_scalar, out=stsl[h].buf[:, 10:11], in0=stsl[h].buf[:, 6:7], scalar1=stsl[h].buf[:, 9:10],
                                     scalar2=-1.0, op0=ALU.mult, op1=ALU.mult, wait=[b4[h]]) for h in range(4)]
                        ons = [on_ring.next() for _ in range(4)]
                        b6 = []
                        for h in range(4):
                            t_ = ACT.op(nc.scalar.activation, out=ons[h].buf[:], in_=k.ps[1 + h][:, :], func=AF.Identity, scale=stsl[h].buf[:, 9:10],
                                        bias=stsl[h].buf[:, 10:11], wait=[b5[h]] + ons[h].free)
                            b6.append(t_)
                            Ofree[h] = [t_]
                            stsl[h].free = [t_]
                        ogs = []
                        for h in range(4):
                            og = og_ring.next()
                            b7 = POOL.op(nc.gpsimd.tensor_tensor, out=og.buf[:], in0=ons[h].buf[:], in1=s1.buf[:, 3072 + h * 512:3072 + (h + 1) * 512],
                                         op=ALU.mult, wait=[b6[h], t1] + og.free)
                            ons[h].free = [b7]
                            ogs.append((og, b7))
                        s1.free = [tOs[-1], ogs[-1][1]]
                        outs[idx] = ogs

                    otst = {}

                    def T_D(idx, h):
                        if idx % 2 == 0 and h == 0:
                            otst["slot"] = ot_ring.next()
                            otst["first"] = True
                        ot = otst["slot"]
                        col = (idx % 2) * 128
                        og, b7 = outs[idx][h]
                        tp = None
                        for wc in range(4):
                            tp = PE.op(nc.tensor.transpose, out=k.ps[5][:, wc * 128:(wc + 1) * 128], in_=og.buf[:, wc * 128:(wc + 1) * 128],
                                       identity=k.ident, wait=[b7] + stt_["PB"] + k.t_init, sig=(wc == 3))
                        og.free = [tp]
                        tev = ACT.op(nc.scalar.copy, out=ot.buf[:, h * 4:(h + 1) * 4, col:col + 128],
                                     in_=k.ps[5][:, :].rearrange("p (w n) -> p w n", n=128), wait=[tp] + (ot.free if otst["first"] else []))
                        otst["first"] = False
                        stt_["PB"] = [tev]
                        otst["tev"] = tev

                    def store_D(idx):
                        outs.pop(idx)
                        if not (idx % 2 == 1 or idx == n - 1):
                            return
                        ot = otst["slot"]
                        i0_ = idx - (idx % 2)
                        ntk = (idx - i0_ + 1) * 128
                        tk0 = chunks[i0_] * 128
                        tst = None
                        for h in range(4):
                            tst = SP.dma(k.ogT[h * 4:(h + 1) * 4, :, tk0:tk0 + ntk].rearrange("w p n -> p w n"), ot.buf[:, h * 4:(h + 1) * 4, 0:ntk],
                                         ot.sem, wait=[otst["tev"]])
                        ot.free = [tst]

                    load_f(0)
                    load_f(1)
                    for it in range(n + 2):
                        do_A = it < n
                        do_C = need_out and 0 <= it - 1 < n
                        do_D = need_out and 0 <= it - 2 < n
                        if do_A:
                            casts_A(it)
                            if need_out:
                                stage_B(it)
                        for h in range(4):
                            if do_A:
                                dS_A(it, h)
                            if do_C:
                                O_C(it - 1, h)
                            if do_D:
                                T_D(it - 2, h)
                        if do_C:
                            chain_C(it - 1)
                        if do_D:
                            store_D(it - 2)
                        load_f(it + 2)
                barrier(k)

            if need_ctx:
                run_seq([32, 33] + list(range(32)), True)
            else:
                run_seq([32, 33], False)
                run_seq(list(range(32)), True)
    barrier(k)


def _consts():
    p = np.arange(128)
    ident = np.eye(128, dtype=np.float32)
    rot = np.zeros((128, 128), np.float32)
    for m in range(128):
        if (m % 64) < 32:
            rot[m + 32, m] = -1.0
        else:
            rot[m - 32, m] = 1.0
    onesdiv = np.full((128, 128), 1.0 / 128, np.float32)
    diff = p[None, :] - p[:, None]
    RP = np.maximum(diff, 0).astype(np.float32)
    RN = np.maximum(-diff, 0).astype(np.float32)
    IDX1 = np.tile((p + 1)[None, :], (128, 1)).astype(np.float32)
    IDXB = np.tile((128 - p)[None, :], (128, 1)).astype(np.float32)
    misc = np.zeros((128, 128), np.float32)
    misc[:, 0] = 127 - p
    misc[:, 1] = p
    misc[:, 2] = 128.0
    misc[:, 3] = 1.0
    misc[:, 4] = QK_EPS
    misc[:, 5] = LN_EPS
    misc[:, 6] = GN_EPS
    cst = np.concatenate([ident, rot, onesdiv, RP, RN, IDX1, IDXB, misc], axis=1)
    t = np.arange(S)
    row = (t // GRID_W).astype(np.float64)
    col = (t % GRID_W).astype(np.float64)
    fr = THETA ** (-(np.arange(32, dtype=np.float64)) / 32)
    angA = np.zeros((128, S))
    for pp in range(128):
        angA[pp] = (row if pp < 64 else col) * fr[pp % 32]
    ropeA = np.stack([np.cos(angA), np.sin(angA)]).astype(np.float32)
    pos = np.concatenate([L + np.arange(S), np.arange(L)]).astype(np.float64)
    frR = THETA ** (-(np.arange(128, dtype=np.float64)) / 128)
    angR = frR[:, None] * pos[None, :]
    ropeR = np.stack([np.cos(angR), np.sin(angR), np.cos(angR) / 16.0, np.sin(angR) / 16.0]).astype(np.float32)
    return cst, ropeA, ropeR


_CACHE = {}


def _host_inputs(inputs):
    f = lambda a: np.ascontiguousarray(np.asarray(a, dtype=np.float32))
    cst, ropeA, ropeR = _consts()
    qk = np.stack([inputs["attn_q_scale"][0], inputs["attn_k_scale"][0], inputs["attn_q_scale"][1], inputs["attn_k_scale"][1]], axis=1)
    gn = np.asarray(inputs["ret_gn_g"]).reshape(2, 16, 128).transpose(2, 0, 1).reshape(128, 32)
    lg = np.stack([np.asarray(inputs["ret_log_decay_fwd"]), np.asarray(inputs["ret_log_decay_bwd"])], axis=1).reshape(1, 16)
    lg = np.tile(lg, (128, 1))
    shared = {
        "mod_w": f(inputs["mod_w"]), "mod_b": f(inputs["mod_b"]), "ln_g": f(inputs["ln_g"]), "ln_b": f(inputs["ln_b"]),
        "attn_w_in": f(inputs["attn_w_in"]), "attn_w_out": f(inputs["attn_w_out"]), "attn_qk": f(qk),
        "ret_w_in": f(inputs["ret_w_in"]), "ret_w_out": f(inputs["ret_w_out"]), "ret_gn": f(gn), "ret_lg": f(lg),
        "cst": f(cst), "ropeA": f(ropeA), "ropeR": f(ropeR),
    }
    maps = []
    cc = np.asarray(inputs["c_ctx"], np.float32).reshape(8, 128).T
    for b in range(8):
        cb = np.asarray(inputs["c"][b], np.float32).reshape(8, 128).T
        cvec = np.stack([cb, cc], axis=2).reshape(128, 16)
        m = dict(shared)
        m["x"] = f(inputs["x"][b])
        m["ctx"] = f(inputs["ctx"][b])
        m["cvec"] = f(cvec)
        maps.append(m)
    return maps


def kernel(**inputs):
    if "nc" not in _CACHE:
        _CACHE["nc"] = build_program()
    nc = _CACHE["nc"]
    maps = _host_inputs(inputs)
    res = run_bass_kernel_spmd(nc, maps, core_ids=list(range(8)))
    return np.stack([np.asarray(r["out"], dtype=np.float32) for r in res.results], axis=0)
```

```python
import contextlib
import os as _os
import numpy as np
import concourse.bass as bass
import concourse.mybir as mybir
from concourse.bass_utils import run_bass_kernel_spmd

F32 = mybir.dt.float32
BF16 = mybir.dt.bfloat16
AF = mybir.ActivationFunctionType
ALU = mybir.AluOpType

D = 1024
S = 4096
L = 256
NT = S + L
DEPTH = 4
ALPHA = (2.0 * DEPTH) ** 0.25
LN_EPS = 1e-5
QK_EPS = 1e-6
GN_EPS = 1e-5
ATT_IN = 2560
RET_IN = 6144
GRID_W = 64
THETA = 10000.0


class Sem:
    def __init__(self, nc, es, name):
        self.sem = es.enter_context(nc.semaphore(name))
        self.cnt = 0


class Eng:
    def __init__(self, nc, es, eng, name):
        self.e = eng
        self.s = Sem(nc, es, "s_" + name)
        self.seen = {}

    def _need(self, toks):
        need = {}
        for t in toks:
            if t is None:
                continue
            s, c = t
            if self.seen.get(s, 0) < c and need.get(s, 0) < c:
                need[s] = c
        return list(need.items())

    def op(self, fn, *a, wait=(), sig=True, **kw):
        items = self._need(wait)
        for s, c in items[:-1]:
            self.e.wait_ge(s.sem, c)
        ins = fn(*a, **kw)
        if items:
            s, c = items[-1]
            ins._wait_ge(s.sem, c)
        for s, c in items:
            self.seen[s] = c
        if sig:
            self.s.cnt += 1
            ins.then_inc(self.s.sem, 1)
            return (self.s, self.s.cnt)
        return None

    def dma(self, out, in_, dsem, wait=()):
        items = self._need(wait)
        for s, c in items:
            self.e.wait_ge(s.sem, c)
            self.seen[s] = c
        ins = self.e.dma_start(out=out, in_=in_)
        dsem.cnt += 16
        ins.then_inc(dsem.sem, 16)
        return (dsem, dsem.cnt)

    def wait_all(self, toks):
        for s, c in self._need(toks):
            self.e.wait_ge(s.sem, c)
            self.seen[s] = c


class Slot:
    def __init__(self, buf, sem=None):
        self.buf = buf
        self.sem = sem
        self.free = []
        self.ready = None


class Ring:
    def __init__(self, slots):
        self.slots = slots
        self.i = 0

    def next(self):
        s = self.slots[self.i % len(self.slots)]
        self.i += 1
        return s


class K:
    pass


def build_program(layers=(0, 1, 2, 3)):
    nc = bass.Bass("TRN2", target_bir_lowering=False)
    k = K()
    k.nc = nc
    es = contextlib.ExitStack()
    k.es = es

    def din(name, shape, dt=F32):
        return nc.dram_tensor(name, list(shape), dt, kind="ExternalInput").ap()

    def dscr(name, shape, dt):
        return nc.dram_tensor(name, list(shape), dt).ap()

    k.x_in = din("x", [S, D])
    k.c_in = din("ctx", [L, D])
    k.cvec = din("cvec", [128, 16])
    k.mod_w = din("mod_w", [DEPTH, D, 3 * D])
    k.mod_b = din("mod_b", [DEPTH, 3 * D])
    k.ln_g = din("ln_g", [DEPTH, D])
    k.ln_b = din("ln_b", [DEPTH, D])
    k.a_w_in = din("attn_w_in", [2, D, ATT_IN])
    k.a_w_out = din("attn_w_out", [2, D, D])
    k.a_qk = din("attn_qk", [128, 4])
    k.r_w_in = din("ret_w_in", [2, D, RET_IN])
    k.r_w_out = din("ret_w_out", [2, 2 * D, D])
    k.r_gn = din("ret_gn", [128, 32])
    k.r_lg = din("ret_lg", [128, 16])
    k.cst = din("cst", [128, 128 * 8])
    k.ropeA = din("ropeA", [2, 128, S])
    k.ropeR = din("ropeR", [4, 128, NT])
    k.out = nc.dram_tensor("out", [S, D], F32, kind="ExternalOutput").ap()
    k.xs = dscr("xs", [S, D], F32)
    k.cs = dscr("cs", [L, D], F32)
    k.qT = dscr("qT", [8, 128, NT], BF16)
    k.gT = dscr("gT", [8, 128, NT], BF16)
    k.ogT = dscr("ogT", [16, 128, NT], BF16)
    k.hTd = dscr("hTd", [9, 128, 8, 512], BF16)
    k.rq = dscr("rq", [3, 8, 128, NT], BF16)
    k.rk = dscr("rk", [8, 128, NT], BF16)
    k.rkz = dscr("rkz", [2, NT, 1024], BF16)
    k.rv = dscr("rv", [NT, 2048], BF16)
    k.rg = dscr("rg", [NT, 2048], BF16)
    k.rsb = dscr("rsb", [34, 128, 8, 512], BF16)

    k.PE = Eng(nc, es, nc.tensor, "pe")
    k.ACT = Eng(nc, es, nc.scalar, "act")
    k.DVE = Eng(nc, es, nc.vector, "dve")
    k.POOL = Eng(nc, es, nc.gpsimd, "pool")
    k.SP = Eng(nc, es, nc.sync, "sp")
    k.nsem = 0
    k.ret_stop = int(_os.environ.get('RET_STOP', '0'))

    def newsem():
        k.nsem += 1
        return Sem(nc, es, f"d{k.nsem}")
    k.newsem = newsem

    k.ps = [es.enter_context(nc.psum_tensor(f"ps{i}", [128, 512], F32)) for i in range(8)]

    def sb(name, shape, dt=F32):
        return es.enter_context(nc.sbuf_tensor(name, list(shape), dt))
    k.sb = sb
    k.cstt = sb("cstt", [128, 1024])
    k.ident = k.cstt[:, 0:128]
    k.rot = k.cstt[:, 128:256]
    k.onesdiv = k.cstt[:, 256:384]
    k.RP = k.cstt[:, 384:512]
    k.RN = k.cstt[:, 512:640]
    k.IDX1 = k.cstt[:, 640:768]
    k.IDXB = k.cstt[:, 768:896]
    k.misc = k.cstt[:, 896:1024]
    k.identb = sb("identb", [128, 128], BF16)
    k.onesb = sb("onesb", [128, 128], BF16)
    k.onesdivb = sb("onesdivb", [128, 128], BF16)
    k.onesf = sb("onesf", [128, 128])
    k.rotb = sb("rotb", [128, 128], BF16)
    k.qk = sb("qk", [128, 4])
    k.gn = sb("gn", [128, 32])
    k.lg = sb("lg", [128, 16])
    k.modT = sb("modT", [128, DEPTH, 48])
    k.m2g = sb("m2g", [2, DEPTH * D])
    k.sel0 = sb("sel0", [2, 128])
    k.sel1 = sb("sel1", [2, 128])

    s0 = newsem()
    t_c = k.SP.dma(k.cstt[:], k.cst[:, :], s0)
    t_c = k.SP.dma(k.qk[:], k.a_qk[:, :], s0)
    t_c = k.SP.dma(k.gn[:], k.r_gn[:, :], s0)
    t_c = k.SP.dma(k.lg[:], k.r_lg[:, :], s0)
    k.t_const = t_c
    t1 = k.DVE.op(nc.vector.tensor_copy, out=k.identb[:], in_=k.ident, wait=[t_c])
    t2 = k.DVE.op(nc.vector.memset, k.onesb[:], 1.0)
    t2 = k.DVE.op(nc.vector.memset, k.onesf[:], 1.0)
    t3 = k.DVE.op(nc.vector.tensor_copy, out=k.onesdivb[:], in_=k.onesdiv, wait=[t_c])
    t3 = k.DVE.op(nc.vector.tensor_copy, out=k.rotb[:], in_=k.rot, wait=[t_c])
    t4 = k.DVE.op(nc.vector.memset, k.sel0[:], 0.0)
    t5 = k.DVE.op(nc.vector.memset, k.sel0[0:1, :], 1.0, wait=[t4])
    t6 = k.DVE.op(nc.vector.memset, k.sel1[:], 1.0)
    t7 = k.DVE.op(nc.vector.memset, k.sel1[0:1, :], 0.0, wait=[t6])
    k.t_init = [t_c, t1, t2, t3, t5, t7]
    k.dram_ready = {}

    k.engines = [k.PE, k.ACT, k.DVE, k.POOL, k.SP]
    k.allsems = [s0]
    k.free_sems = []
    k.borrowed = []

    def borrow():
        if k.free_sems:
            sm = k.free_sems.pop()
        else:
            sm = newsem()
            k.allsems.append(sm)
        k.borrowed.append(sm)
        return sm
    k.borrow = borrow
    k.free_sw = []
    k.borrowed_sw = []

    def borrow_sw():
        if k.free_sw:
            sm = k.free_sw.pop()
        else:
            sm = newsem()
            k.allsems.append(sm)
        k.borrowed_sw.append(sm)
        return sm
    k.borrow_sw = borrow_sw

    emit_modulation(k)
    barrier(k)
    x_src, c_src = k.x_in, k.c_in
    for i in layers:
        last = (i == DEPTH - 1)
        x_dst = k.out if last else k.xs
        need_ctx = not last
        if i % 2 == 0:
            attn_layer(k, i, x_src, c_src)
            outproj_ln(k, i, x_src, c_src, x_dst, k.cs, 8, k.a_w_out[i // 2], None, need_ctx)
        else:
            ret_layer(k, i, x_src, c_src, need_ctx)
            if k.ret_stop:
                continue
            outproj_ln(k, i, x_src, c_src, x_dst, k.cs, 16, k.r_w_out[i // 2], i // 2, need_ctx)
        x_src, c_src = x_dst, k.cs
    if tuple(layers) != (0, 1, 2, 3):
        dbg_c = nc.dram_tensor("dbg_c", [L, D], F32, kind="ExternalOutput").ap()
        sd_ = k.borrow()
        if x_src is not k.out:
            k.SP.dma(k.out[:, :], x_src[:, :], sd_)
        k.SP.dma(dbg_c[:, :], k.cs[:, :], sd_)
        barrier(k)
    es.close()
    return nc


def barrier(k):
    toks = [(E.s, E.s.cnt) for E in k.engines] + [(sm, sm.cnt) for sm in k.allsems]
    toks = [t for t in toks if t[1] > 0]
    for E in k.engines:
        E.wait_all(toks)
    k.free_sems.extend(k.borrowed)
    k.borrowed = []
    k.free_sw.extend(k.borrowed_sw)
    k.borrowed_sw = []


class Phase:
    def __init__(self, k, tag):
        self.k = k
        self.tag = tag
        self.es = contextlib.ExitStack()
        self.n = 0

    def __enter__(self):
        self.es.__enter__()
        return self

    def __exit__(self, *a):
        return self.es.__exit__(*a)

    def sb(self, shape, dt=F32, name=None):
        self.n += 1
        return self.es.enter_context(self.k.nc.sbuf_tensor(f"{self.tag}_{name or 't'}{self.n}", list(shape), dt))

    def ring(self, n, shape, dt=F32, dma=False, name=None):
        mk = (self.k.borrow_sw if dma == "sw" else self.k.borrow) if dma else None
        return Ring([Slot(self.sb(shape, dt, name), mk() if mk else None) for _ in range(n)])


def emit_modulation(k):
    nc = k.nc
    PE, ACT, DVE, POOL, SP = k.engines
    with Phase(k, "mod") as ph:
        cv = ph.sb([128, 16])
        cs_ = ph.sb([128, 16])
        mb = ph.sb([2, DEPTH * 3 * D])
        k.m2 = ph.sb([2, DEPTH * 3 * D])
        wring = ph.ring(4, [128, 8, 512], F32, dma=True)
        s1 = k.borrow()
        SP.dma(cv[:], k.cvec[:, :], s1)
        mbflat = k.mod_b.rearrange("(o i) n -> o (i n)", o=1)
        SP.dma(mb[0:1, :], mbflat, s1)
        t_ld = SP.dma(mb[1:2, :], mbflat, s1)
        t_silu = ACT.op(nc.scalar.activation, out=cs_[:], in_=cv[:], func=AF.Silu, wait=[t_ld])
        t_mb = None
        for i in range(DEPTH):
            o = i * 3 * D + D
            t_mb = DVE.op(nc.vector.tensor_scalar, out=mb[0:2, o:o + D], in0=mb[0:2, o:o + D], scalar1=1.0, scalar2=None,
                          op0=ALU.add, wait=[t_ld])
        bank_free = [[], []]
        cnt = 0
        for i in range(DEPTH):
            t_ev = None
            for n in range(6):
                slot = wring.next()
                t_w = (SP if cnt % 2 == 0 else ACT).dma(slot.buf[:], k.mod_w[i][:, n * 512:(n + 1) * 512].rearrange("(kk p) n -> p kk n", p=128),
                                                        slot.sem, wait=slot.free)
                b = cnt % 2
                cnt += 1
                t_mm = None
                for kk in range(8):
                    t_mm = PE.op(nc.tensor.matmul, k.ps[b][0:2, 0:512], lhsT=cs_[:, 2 * kk:2 * kk + 2], rhs=slot.buf[:, kk, :],
                                 start=(kk == 0), stop=(kk == 7), wait=[t_w, t_silu] + bank_free[b], sig=(kk == 7))
                slot.free = [t_mm]
                off = i * 3 * D + n * 512
                t_ev = DVE.op(nc.vector.tensor_tensor, out=k.m2[0:2, off:off + 512], in0=k.ps[b][0:2, 0:512],
                              in1=mb[0:2, off:off + 512], op=ALU.add, wait=[t_mm, t_mb])
                bank_free[b] = [t_ev]
            t_c = None
            for j in range(24):
                o = i * 3 * D + j * 128
                t_c = PE.op(nc.tensor.matmul, k.ps[2][:, 2 * j:2 * j + 2], lhsT=k.m2[0:2, o:o + 128], rhs=k.ident[0:2, 0:2],
                            start=True, stop=True, wait=[t_ev] + k.t_init + bank_free[0] * 0, sig=(j == 23))
            t_cp = DVE.op(nc.vector.tensor_copy, out=k.modT[:, i, :], in_=k.ps[2][:, 0:48], wait=[t_c])
            DVE.op(nc.vector.tensor_copy, out=k.m2g[0:2, i * D:(i + 1) * D], in_=k.m2[0:2, i * 3 * D + 2 * D:(i + 1) * 3 * D], wait=[t_ev])
            PE.wait_all([t_cp])


def attn_layer(k, i, x_src, c_src):
    nc = k.nc
    PE, ACT, DVE, POOL, SP = k.engines
    j = i // 2
    with Phase(k, f"al{i}") as lay:
        kT = lay.sb([128, 2, NT], BF16, "kT")
        V = lay.sb([128, NT // 128, 256], BF16, "V")
        with Phase(k, f"ai{i}") as ph:
            wbf = ph.sb([128, 8, ATT_IN], BF16, "wbf")
            wsem = k.borrow_sw()
            t_w = None
            for kk in range(8):
                t_w = POOL.dma(wbf[:, kk, :], k.a_w_in[j][kk * 128:(kk + 1) * 128, :], wsem)
            xs_ring = ph.ring(2, [128, 4, D], F32, dma=True, name="xs")
            hT_ring = ph.ring(2, [128, 8, 512], BF16, name="hT")
            rp_ring = ph.ring(2, [128, 2, 512], F32, dma=True, name="rope")
            sq_ring = ph.ring(2, [128, 512], BF16, name="sq")
            sd_ring = ph.ring(2, [128, 512], F32, name="sd")
            rs_ring = ph.ring(2, [128, 512], F32, name="rstd")
            xg_ring = ph.ring(3, [128, 512], BF16, name="xg")
            m1_ring = ph.ring(2, [128, 512], F32, name="m1")
            m2_ring = ph.ring(2, [128, 512], F32, name="m2")
            qst_ring = ph.ring(3, [128, 512], BF16, dma=True, name="qst")
            gst_ring = ph.ring(3, [128, 512], BF16, dma=True, name="gst")
            blocks = [(x_src[tb * 512:(tb + 1) * 512, :], tb * 512, 512, 0) for tb in range(8)] + [(c_src[0:L, :], S, L, 1)]
            loads = {}

            def issue_load(b):
                if b >= len(blocks):
                    return
                src, off, nt, r = blocks[b]
                slot = xs_ring.next()
                t = SP.dma(slot.buf[:, 0:nt // 128, :], src.rearrange("(t p) d -> p t d", p=128), slot.sem, wait=slot.free)
                rp = None
                t_r = None
                if r == 0:
                    rp = rp_ring.next()
                    SP.dma(rp.buf[:, 0, :], k.ropeA[0][:, off:off + 512], rp.sem, wait=rp.free)
                    t_r = SP.dma(rp.buf[:, 1, :], k.ropeA[1][:, off:off + 512], rp.sem)
                loads[b] = (slot, t, rp, t_r)

            issue_load(0)
            Tfree = [[], []]
            Afree = [[], [], []]
            Bfree = []
            Cfree = []
            na = 0
            eps_t = k.misc[:, 4:5]
            for b, (src, off, nt, r) in enumerate(blocks):
                issue_load(b + 1)
                slot, t_x, rp, t_r = loads.pop(b)
                hs = hT_ring.next()
                nt4 = nt // 128
                t_ev = None
                tT = None
                for kk in range(8):
                    T = k.ps[kk % 2]
                    for tt in range(nt4):
                        tT = PE.op(nc.tensor.transpose, out=T[:, tt * 128:(tt + 1) * 128], in_=slot.buf[:, tt, kk * 128:(kk + 1) * 128],
                                   identity=k.ident, wait=[t_x] + Tfree[kk % 2] + k.t_init, sig=(tt == nt4 - 1))
                    t_ev = ACT.op(nc.scalar.activation, out=hs.buf[:, kk, 0:nt], in_=T[:, 0:nt], func=AF.Identity,
                                  scale=k.modT[:, i, (8 + kk) * 2 + r:(8 + kk) * 2 + r + 1],
                                  bias=k.modT[:, i, kk * 2 + r:kk * 2 + r + 1], wait=[tT] + hs.free)
                    Tfree[kk % 2] = [t_ev]
                slot.free = [tT]
                pend = []

                def proj_fm(oc):
                    nonlocal na
                    A = k.ps[2 + na % 3]
                    ai = na % 3
                    na += 1
                    tA = None
                    for kk in range(8):
                        tA = PE.op(nc.tensor.matmul, A[:, 0:nt], lhsT=wbf[:, kk, oc * 128:(oc + 1) * 128], rhs=hs.buf[:, kk, 0:nt],
                                   start=(kk == 0), stop=(kk == 7), wait=[t_ev, t_w] + Afree[ai], sig=(kk == 7))
                    return A, ai, tA

                def qk_chunk(oc, is_k, hidx):
                    nonlocal Bfree, Cfree
                    A, ai, tA = proj_fm(oc)
                    gcol = k.qk[:, 2 * j + (1 if is_k else 0):2 * j + (1 if is_k else 0) + 1]
                    sq = sq_ring.next()
                    t1 = ACT.op(nc.scalar.activation, out=sq.buf[:, 0:nt], in_=A[:, 0:nt], func=AF.Square, wait=[tA] + sq.free)
                    st = {}

                    def stage2():
                        nonlocal Bfree
                        Bk = k.ps[5]
                        t2 = PE.op(nc.tensor.matmul, Bk[:, 0:nt], lhsT=k.onesdivb[:], rhs=sq.buf[:, 0:nt], start=True, stop=True,
                                   wait=[t1] + Bfree + k.t_init)
                        sq.free = [t2]
                        sd = sd_ring.next()
                        t3 = ACT.op(nc.scalar.activation, out=sd.buf[:, 0:nt], in_=Bk[:, 0:nt], func=AF.Ln, bias=eps_t,
                                    wait=[t2] + sd.free)
                        Bfree = [t3]
                        rs = rs_ring.next()
                        t4 = ACT.op(nc.scalar.activation, out=rs.buf[:, 0:nt], in_=sd.buf[:, 0:nt], func=AF.Exp, scale=-0.5,
                                    wait=[t3] + rs.free)
                        sd.free = [t4]
                        xg = xg_ring.next()
                        t5 = DVE.op(nc.vector.scalar_tensor_tensor, out=xg.buf[:, 0:nt], in0=A[:, 0:nt], scalar=gcol,
                                    in1=rs.buf[:, 0:nt], op0=ALU.mult, op1=ALU.mult, wait=[t4, tA] + xg.free)
                        rs.free = [t5]
                        Afree[ai] = [t1, t5]
                        st["xg"] = xg
                        st["t5"] = t5

                    def stage3():
                        nonlocal Cfree
                        xg = st["xg"]
                        t5 = st["t5"]
                        if is_k:
                            dest = kT[:, hidx, off:off + nt]
                            qs_ = None
                        else:
                            qs_ = qst_ring.next()
                            dest = qs_.buf[:, 0:nt]
                        if r == 0:
                            Ck = k.ps[6]
                            t6 = PE.op(nc.tensor.matmul, Ck[:, 0:nt], lhsT=k.rotb[:], rhs=xg.buf[:, 0:nt], start=True, stop=True,
                                       wait=[t5] + Cfree + k.t_init)
                            m1 = m1_ring.next()
                            t7 = POOL.op(nc.gpsimd.tensor_tensor, out=m1.buf[:, 0:nt], in0=xg.buf[:, 0:nt], in1=rp.buf[:, 0, 0:nt],
                                         op=ALU.mult, wait=[t5, t_r] + m1.free)
                            xg.free = [t6, t7]
                            m2 = m2_ring.next()
                            t8 = DVE.op(nc.vector.tensor_tensor, out=m2.buf[:, 0:nt], in0=Ck[:, 0:nt], in1=rp.buf[:, 1, 0:nt],
                                        op=ALU.mult, wait=[t6, t_r] + m2.free)
                            Cfree = [t8]
                            t9 = POOL.op(nc.gpsimd.tensor_tensor, out=dest, in0=m1.buf[:, 0:nt], in1=m2.buf[:, 0:nt], op=ALU.add,
                                         wait=[t7, t8] + (qs_.free if qs_ else []))
                            m1.free = [t9]
                            m2.free = [t9]
                            st["rp_done"] = t9
                        else:
                            t9 = POOL.op(nc.gpsimd.tensor_copy, out=dest, in_=xg.buf[:, 0:nt], wait=[t5] + (qs_.free if qs_ else []))
                            xg.free = [t9]
                        if qs_ is not None:
                            t10 = SP.dma(k.qT[hidx][:, off:off + nt], qs_.buf[:, 0:nt], qs_.sem, wait=[t9])
                            qs_.free = [t10]
                        st["t9"] = t9
                    pend.append([stage2, stage3])

                def advance():
                    if len(pend) >= 2 and pend[-2] and len(pend[-2]) == 2:
                        pend[-2].pop(0)()
                    if len(pend) >= 3 and pend[-3] and len(pend[-3]) == 1:
                        pend[-3].pop(0)()

                def flush():
                    for p in pend:
                        while p:
                            p.pop(0)()

                for kv in range(2):
                    qk_chunk(16 + kv, True, kv)
                    advance()
                for tt in range(nt4):
                    A = k.ps[2 + na % 3]
                    ai = na % 3
                    na += 1
                    tA = None
                    for kk in range(8):
                        tA = PE.op(nc.tensor.matmul, A[:, 0:256], lhsT=hs.buf[:, kk, tt * 128:(tt + 1) * 128], rhs=wbf[:, kk, 2304:2560],
                                   start=(kk == 0), stop=(kk == 7), wait=[t_ev, t_w] + Afree[ai], sig=(kk == 7))
                    tv = DVE.op(nc.vector.tensor_copy, out=V[:, off // 128 + tt, :], in_=A[:, 0:256], wait=[tA])
                    Afree[ai] = [tv]
                    pend.append([])
                    advance()
                for h in range(8):
                    qk_chunk(h, False, h)
                    advance()
                for h in range(8):
                    A, ai, tA = proj_fm(8 + h)
                    gs_ = gst_ring.next()
                    tg = ACT.op(nc.scalar.activation, out=gs_.buf[:, 0:nt], in_=A[:, 0:nt], func=AF.Silu, wait=[tA] + gs_.free)
                    Afree[ai] = [tg]
                    tgd = SP.dma(k.gT[h][:, off:off + nt], gs_.buf[:, 0:nt], gs_.sem, wait=[tg])
                    gs_.free = [tgd]
                    pend.append([])
                    advance()
                flush()
                hs.free = [(PE.s, PE.s.cnt)]
                if rp is not None:
                    rp.free = [(POOL.s, POOL.s.cnt), (DVE.s, DVE.s.cnt)]
        barrier(k)
        with Phase(k, f"ac{i}") as ph:
            q_ring = ph.ring(2, [128, 8, 512], BF16, dma=True, name="q")
            g_ring = ph.ring(2, [128, 8, 512], BF16, dma=True, name="g")
            pt_ring = ph.ring(8, [128, 512], BF16, name="pt")
            ri_ring = ph.ring(2, [128, 512], F32, name="ri")
            tm_ring = ph.ring(2, [128, 512], F32, name="tm")
            og_ring = ph.ring(3, [128, 512], BF16, dma=True, name="og")
            acc_ring = ph.ring(2, [128, 512], F32, name="acc")
            accs = {}
            blocks = [(tb * 512, 512, list(range(NT // 128))) for tb in range(8)] + [(S, L, [32, 33])]
            loads = {}

            def issue_load2(b):
                if b >= len(blocks):
                    return
                off, nq, kts = blocks[b]
                qs_ = q_ring.next()
                tq = SP.dma(qs_.buf[:, :, 0:nq], k.qT[:, :, off:off + nq].rearrange("h p n -> p h n"), qs_.sem, wait=qs_.free)
                gs_ = g_ring.next()
                tg = SP.dma(gs_.buf[:, :, 0:nq], k.gT[:, :, off:off + nq].rearrange("h p n -> p h n"), gs_.sem, wait=gs_.free)
                loads[b] = (qs_, tq, gs_, tg)

            Sfree = [[], [], [], []]
            SB = [0, 1, 2, 7]
            Ofree = [[], []]
            steps = []
            grp = 0
            for b, (off, nq, kts) in enumerate(blocks):
                for h in range(8):
                    for kt in kts:
                        steps.append(dict(b=b, off=off, nq=nq, h=h, kt=kt, j=kts.index(kt), first=(kt == kts[0]), last=(kt == kts[-1]), grp=grp,
                                          blast=(h == 7 and kt == kts[-1])))
                    grp += 1
            scale = float(128 ** -0.5)
            cur = {}

            def emit_S(idx, st):
                if st["h"] == 0 and st["first"]:
                    if st["b"] == 0:
                        issue_load2(0)
                        issue_load2(1)
                    cur[st["b"]] = loads.pop(st["b"])
                qs_, tq, gs_, tg = cur[st["b"]]
                nq = st["nq"]
                Sb = k.ps[SB[idx % 4]]
                kvh = st["h"] // 4
                tS = PE.op(nc.tensor.matmul, Sb[:, 0:nq], lhsT=kT[:, kvh, st["kt"] * 128:(st["kt"] + 1) * 128], rhs=qs_.buf[:, st["h"], 0:nq],
                           start=True, stop=True, wait=[tq] + Sfree[idx % 4])
                if st["blast"]:
                    qs_.free = [tS]
                pt = pt_ring.next()
                tE = ACT.op(nc.scalar.activation, out=pt.buf[:, 0:nq], in_=Sb[:, 0:nq], func=AF.Exp, scale=scale, wait=[tS] + pt.free)
                Sfree[idx % 4] = [tE]
                st["pt"] = pt
                st["tE"] = tE

            def emit_PV(st):
                qs_, tq, gs_, tg = cur[st["b"]]
                nq = st["nq"]
                g2 = st["grp"] % 2
                Ob = k.ps[3 + g2]
                Rb = k.ps[5 + g2]
                kvh = st["h"] // 4
                pt = st["pt"]
                j_ = st["j"]
                if j_ % 2 == 0:
                    PE.op(nc.tensor.matmul, Ob[:, 0:nq], lhsT=V[:, st["kt"], kvh * 128:(kvh + 1) * 128], rhs=pt.buf[:, 0:nq],
                          start=st["first"], stop=st["last"], wait=[st["tE"]] + (Ofree[g2] if st["first"] else []), sig=False)
                    tP = PE.op(nc.tensor.matmul, Rb[:, 0:nq], lhsT=k.onesb[:], rhs=pt.buf[:, 0:nq], start=st["first"], stop=False,
                               wait=k.t_init)
                    pt.free = [tP]
                else:
                    tP = PE.op(nc.tensor.matmul, Ob[:, 0:nq], lhsT=V[:, st["kt"], kvh * 128:(kvh + 1) * 128], rhs=pt.buf[:, 0:nq],
                               start=st["first"], stop=st["last"], wait=[st["tE"]])
                    if j_ == 1:
                        accs[st["grp"]] = acc_ring.next()
                        acc = accs[st["grp"]]
                        ta = DVE.op(nc.vector.tensor_copy, out=acc.buf[:, 0:nq], in_=pt.buf[:, 0:nq], wait=[st["tE"]] + acc.free)
                    else:
                        acc = accs[st["grp"]]
                        ta = DVE.op(nc.vector.tensor_tensor, out=acc.buf[:, 0:nq], in0=acc.buf[:, 0:nq], in1=pt.buf[:, 0:nq], op=ALU.add,
                                    wait=[st["tE"]])
                    pt.free = [tP, ta]
                if st["last"]:
                    acc = accs.pop(st["grp"])
                    tP = PE.op(nc.tensor.matmul, Rb[:, 0:nq], lhsT=k.onesf[:], rhs=acc.buf[:, 0:nq], start=False, stop=True,
                               wait=[ta] + k.t_init)
                    acc.free = [tP]
                    ri = ri_ring.next()
                    t1 = DVE.op(nc.vector.reciprocal, out=ri.buf[:, 0:nq], in_=Rb[:, 0:nq], wait=[tP] + ri.free)
                    tm = tm_ring.next()
                    t2 = DVE.op(nc.vector.tensor_tensor, out=tm.buf[:, 0:nq], in0=Ob[:, 0:nq], in1=ri.buf[:, 0:nq], op=ALU.mult,
                                wait=[t1] + tm.free)
                    ri.free = [t2]
                    Ofree[g2] = [t2]
                    og = og_ring.next()
                    t3 = POOL.op(nc.gpsimd.tensor_tensor, out=og.buf[:, 0:nq], in0=tm.buf[:, 0:nq], in1=gs_.buf[:, st["h"], 0:nq],
                                 op=ALU.mult, wait=[t2, tg] + og.free)
                    tm.free = [t3]
                    t4 = SP.dma(k.ogT[st["h"]][:, st["off"]:st["off"] + nq], og.buf[:, 0:nq], og.sem, wait=[t3])
                    og.free = [t4]
                    if st["blast"]:
                        gs_.free = [t3]
                        issue_load2(st["b"] + 2)

            LAG = 3
            for idx, st in enumerate(steps):
                emit_S(idx, st)
                if idx >= LAG:
                    emit_PV(steps[idx - LAG])
            for st in steps[-LAG:]:
                emit_PV(st)
    barrier(k)


def outproj_ln(k, i, x_src, c_src, x_dst, c_dst, nW, w_out, gn_idx, need_ctx):
    nc = k.nc
    PE, ACT, DVE, POOL, SP = k.engines
    with Phase(k, f"op{i}") as ph:
        wbf = ph.sb([128, nW, D], BF16, "wbf")
        t_w = None
        if gn_idx is None:
            wsem = k.borrow_sw()
            for wc in range(nW):
                t_w = POOL.dma(wbf[:, wc, :], w_out[wc * 128:(wc + 1) * 128, :], wsem)
        else:
            stg = ph.ring(2, [128, D], F32, dma=True, name="wstg")
            for wc in range(nW):
                sl = stg.next()
                tl = SP.dma(sl.buf[:], w_out[wc * 128:(wc + 1) * 128, :], sl.sem, wait=sl.free)
                t_w = DVE.op(nc.vector.tensor_scalar, out=wbf[:, wc, :], in0=sl.buf[:], scalar1=k.gn[:, gn_idx * 16 + wc:gn_idx * 16 + wc + 1],
                             scalar2=None, op0=ALU.mult, wait=[tl] + k.t_init)
                sl.free = [t_w]
        gate_bc = ph.sb([128, 2, D], F32, "gate")
        ln_bc = ph.sb([128, 2, D], F32, "lnbc")
        lnrow = ph.sb([1, 2 * D], F32, "lnrow")
        s1 = k.borrow()
        SP.dma(lnrow[0:1, 0:D], k.ln_g[i:i + 1, :], s1)
        t_ln = SP.dma(lnrow[0:1, D:2 * D], k.ln_b[i:i + 1, :], s1)
        bfree = [[], []]
        cnt = 0
        t_bc = None
        for r in range(2):
            for nh in range(2):
                bnk = k.ps[cnt % 2]
                o = i * D + nh * 512
                t = PE.op(nc.tensor.matmul, bnk[:, :], lhsT=(k.sel0 if r == 0 else k.sel1)[:], rhs=k.m2g[0:2, o:o + 512], start=True, stop=True,
                          wait=k.t_init + bfree[cnt % 2])
                t_bc = ACT.op(nc.scalar.copy, out=gate_bc[:, r, nh * 512:(nh + 1) * 512], in_=bnk[:, :], wait=[t])
                bfree[cnt % 2] = [t_bc]
                cnt += 1
        for q in range(2):
            for nh in range(2):
                bnk = k.ps[cnt % 2]
                o = q * D + nh * 512
                t = PE.op(nc.tensor.matmul, bnk[:, :], lhsT=k.sel0[0:1, :], rhs=lnrow[0:1, o:o + 512], start=True, stop=True,
                          wait=k.t_init + [t_ln] + bfree[cnt % 2])
                t_bc = ACT.op(nc.scalar.copy, out=ln_bc[:, q, nh * 512:(nh + 1) * 512], in_=bnk[:, :], wait=[t])
                bfree[cnt % 2] = [t_bc]
                cnt += 1
        og_ring = ph.ring(2, [128, nW, 512], BF16, dma=True, name="og")
        xs_ring = ph.ring(2, [128, 4, D], F32, dma=True, name="xs")
        t_ring = ph.ring(3, [128, D], F32, name="t")
        u_ring = ph.ring(2, [128, D], F32, name="u")
        o_ring = ph.ring(3, [128, D], F32, dma=True, name="o")
        st_ring = ph.ring(3, [128, 16], F32, name="st")
        blocks = [(x_src, x_dst, tb * 512, tb * 512, 512, 0) for tb in range(8)]
        if need_ctx:
            blocks.append((c_src, c_dst, 0, S, L, 1))
        loads = {}

        def issue_load(b):
            if b >= len(blocks):
                return
            src, dst, row0, off, nt, r = blocks[b]
            ogs = og_ring.next()
            t_og = SP.dma(ogs.buf[:, :, 0:nt], k.ogT[0:nW, :, off:off + nt].rearrange("w p n -> p w n"), ogs.sem, wait=ogs.free)
            xs_ = xs_ring.next()
            t_x = SP.dma(xs_.buf[:, 0:nt // 128, :], src[row0:row0 + nt, :].rearrange("(t p) d -> p t d", p=128), xs_.sem, wait=xs_.free)
            loads[b] = (ogs, t_og, xs_, t_x)

        tails = []

        def tail_body(tb_, sts, t1, ta, row0, tt, dst):
            t2 = DVE.op(nc.vector.bn_aggr, out=sts.buf[:, 12:14], in_=sts.buf[:, 0:12], wait=[t1])
            t3 = ACT.op(nc.scalar.activation, out=sts.buf[:, 14:15], in_=sts.buf[:, 13:14], func=AF.Sqrt, bias=eps_ln, wait=[t2])
            t4 = DVE.op(nc.vector.reciprocal, out=sts.buf[:, 15:16], in_=sts.buf[:, 14:15], wait=[t3])
            t5 = DVE.op(nc.vector.tensor_scalar, out=sts.buf[:, 14:15], in0=sts.buf[:, 12:13], scalar1=sts.buf[:, 15:16], scalar2=-1.0,
                        op0=ALU.mult, op1=ALU.mult, wait=[t4])
            ub = u_ring.next()
            t6 = ACT.op(nc.scalar.activation, out=ub.buf[:], in_=tb_.buf[:], func=AF.Identity, scale=sts.buf[:, 15:16],
                        bias=sts.buf[:, 14:15], wait=[t5, ta] + ub.free)
            tb_.free = [t6]
            sts.free = [t6]
            t7 = POOL.op(nc.gpsimd.tensor_tensor, out=ub.buf[:], in0=ub.buf[:], in1=ln_bc[:, 0, :], op=ALU.mult, wait=[t6, t_bc])
            ob = o_ring.next()
            t8 = POOL.op(nc.gpsimd.tensor_tensor, out=ob.buf[:], in0=ub.buf[:], in1=ln_bc[:, 1, :], op=ALU.add, wait=[t7] + ob.free)
            ub.free = [t8]
            r0 = row0 + tt * 128
            t9 = SP.dma(dst[r0:r0 + 128, :], ob.buf[:], ob.sem, wait=[t8])
            ob.free = [t9]

        issue_load(0)
        Yfree = [[], [], [], []]
        yc = 0
        eps_ln = k.misc[:, 5:6]
        for b, (src, dst, row0, off, nt, r) in enumerate(blocks):
            issue_load(b + 1)
            ogs, t_og, xs_, t_x = loads.pop(b)
            for tt in range(nt // 128):
                tb_ = t_ring.next()
                tys = []
                for nh in range(2):
                    Y = k.ps[3 + yc % 4]
                    yi = yc % 4
                    yc += 1
                    tY = None
                    for wc in range(nW):
                        tY = PE.op(nc.tensor.matmul, Y[:, :], lhsT=ogs.buf[:, wc, tt * 128:(tt + 1) * 128], rhs=wbf[:, wc, nh * 512:(nh + 1) * 512],
                                   start=(wc == 0), stop=(wc == nW - 1), wait=[t_og, t_w] + Yfree[yi], sig=(wc == nW - 1))
                    ty = DVE.op(nc.vector.tensor_tensor, out=tb_.buf[:, nh * 512:(nh + 1) * 512], in0=Y[:, :],
                                in1=gate_bc[:, r, nh * 512:(nh + 1) * 512], op=ALU.mult, wait=[tY, t_bc] + tb_.free)
                    Yfree[yi] = [ty]
                    tys.append(ty)
                ta = DVE.op(nc.vector.scalar_tensor_tensor, out=tb_.buf[:], in0=xs_.buf[:, tt, :], scalar=float(ALPHA), in1=tb_.buf[:],
                            op0=ALU.mult, op1=ALU.add, wait=tys + [t_x])
                sts = st_ring.next()
                DVE.op(nc.vector.bn_stats, out=sts.buf[:, 0:6], in_=tb_.buf[:, 0:512], wait=[ta] + sts.free, sig=False)
                t1 = DVE.op(nc.vector.bn_stats, out=sts.buf[:, 6:12], in_=tb_.buf[:, 512:1024])
                while tails:
                    tails.pop(0)()

                def tail(tb_=tb_, sts=sts, t1=t1, ta=ta, row0=row0, tt=tt, dst=dst):
                    tail_body(tb_, sts, t1, ta, row0, tt, dst)
                tails.append(tail)
            ogs.free = [(PE.s, PE.s.cnt)]
            xs_.free = [(DVE.s, DVE.s.cnt)]
        while tails:
            tails.pop(0)()
    barrier(k)


def ret_layer(k, i, x_src, c_src, need_ctx):
    nc = k.nc
    PE, ACT, DVE, POOL, SP = k.engines
    j = i // 2
    lgc = lambda d, h: k.lg[:, j * 8 + d * 4 + h:j * 8 + d * 4 + h + 1]
    blocks = [(x_src[tb * 512:(tb + 1) * 512, :], tb * 512, 512, 0) for tb in range(8)] + [(c_src[0:L, :], S, L, 1)]
    with Phase(k, f"rl{i}") as lay:
        xi = lay.sb([128, 8, 128], F32, "xi")
        zc = lay.sb([128, 16], F32, "zc")
        DT = lay.sb([128, 4, 128], F32, "DT")
        tmpd = lay.sb([128, 128], F32, "tmpd")
        t_dec = []
        for h in range(4):
            t_dec.append(ACT.op(nc.scalar.activation, out=xi[:, h * 2, :], in_=k.IDX1, func=AF.Exp, scale=lgc(0, h), wait=k.t_init))
            t_dec.append(ACT.op(nc.scalar.activation, out=xi[:, h * 2 + 1, :], in_=k.IDXB, func=AF.Exp, scale=lgc(1, h)))
            ACT.op(nc.scalar.activation, out=zc[:, h:h + 1], in_=k.misc[:, 0:1], func=AF.Exp, scale=lgc(0, h))
            ACT.op(nc.scalar.activation, out=zc[:, 4 + h:5 + h], in_=k.misc[:, 1:2], func=AF.Exp, scale=lgc(1, h))
            ACT.op(nc.scalar.activation, out=zc[:, 8 + h:9 + h], in_=k.misc[:, 2:3], func=AF.Exp, scale=lgc(0, h))
            t_dec.append(ACT.op(nc.scalar.activation, out=zc[:, 12 + h:13 + h], in_=k.misc[:, 2:3], func=AF.Exp, scale=lgc(1, h)))
            ta = DVE.op(nc.vector.tensor_scalar, out=tmpd[:], in0=k.RP, scalar1=lgc(0, h), scalar2=None, op0=ALU.mult,
                        wait=k.t_init + t_dec[-4:])
            tb2 = DVE.op(nc.vector.scalar_tensor_tensor, out=tmpd[:], in0=k.RN, scalar=lgc(1, h), in1=tmpd[:], op0=ALU.mult, op1=ALU.add,
                         wait=[ta])
            t_dec.append(ACT.op(nc.scalar.activation, out=DT[:, h, :], in_=tmpd[:], func=AF.Exp, wait=[tb2]))
        t_dec = [t_dec[-1], t_dec[-2]]
        with Phase(k, f"ra{i}") as ph:
            wbf = ph.sb([128, 8, 2048], BF16, "wbf")
            wsem = k.borrow_sw()
            t_w = None
            for kk in range(8):
                t_w = POOL.dma(wbf[:, kk, :], k.r_w_in[j][kk * 128:(kk + 1) * 128, 0:2048], wsem)
            xs_ring = ph.ring(2, [128, 4, D], F32, dma=True, name="xs")
            hT_ring = ph.ring(2, [128, 8, 512], BF16, dma=True, name="hT")
            rp_ring = ph.ring(2, [128, 4, 512], F32, dma=True, name="rope")
            tt_ring = [ph.ring(2, [128, 512], F32, name=f"t{q}") for q in range(4)]
            o_ring = [ph.ring(2, [128, 512], F32, name=f"o{q}") for q in range(2)]
            qst_ring = ph.ring(4, [128, 3, 512], BF16, dma=True, name="qst")
            kst_ring = ph.ring(2, [128, 2, 512], BF16, dma=True, name="kst")
            kz_ring = ph.ring(2, [128, 4, 2, 256], BF16, dma=True, name="kz")
            loads = {}

            def issue_load(b):
                if b >= len(blocks):
                    return
                src, off, nt, r = blocks[b]
                slot = xs_ring.next()
                t = SP.dma(slot.buf[:, 0:nt // 128, :], src.rearrange("(t p) d -> p t d", p=128), slot.sem, wait=slot.free)
                rp = rp_ring.next()
                t_r = SP.dma(rp.buf[:, :, 0:nt], k.ropeR[:, :, off:off + nt].rearrange("c p n -> p c n"), rp.sem, wait=rp.free)
                loads[b] = (slot, t, rp, t_r)

            issue_load(0)
            Tfree = [[], []]
            Afree = [[], [], [], []]
            Pfree = []
            kz_pending = []
            na = 0
            for b, (src, off, nt, r) in enumerate(blocks):
                issue_load(b + 1)
                slot, t_x, rp, t_r = loads.pop(b)
                hs = hT_ring.next()
                nt4 = nt // 128
                t_ev = None
                tT = None
                for kk in range(8):
                    T = k.ps[kk % 2]
                    for tt in range(nt4):
                        tT = PE.op(nc.tensor.transpose, out=T[:, tt * 128:(tt + 1) * 128], in_=slot.buf[:, tt, kk * 128:(kk + 1) * 128],
                                   identity=k.ident, wait=[t_x] + Tfree[kk % 2] + k.t_init, sig=(tt == nt4 - 1))
                    t_ev = ACT.op(nc.scalar.activation, out=hs.buf[:, kk, 0:nt], in_=T[:, 0:nt], func=AF.Identity,
                                  scale=k.modT[:, i, (8 + kk) * 2 + r:(8 + kk) * 2 + r + 1],
                                  bias=k.modT[:, i, kk * 2 + r:kk * 2 + r + 1], wait=[tT] + hs.free)
                    Tfree[kk % 2] = [t_ev]
                slot.free = [tT]
                t_hst = SP.dma(k.hTd[b][:, :, 0:nt], hs.buf[:, :, 0:nt], hs.sem, wait=[t_ev])
                ra_mode = int(_os.environ.get('RA_MODE', '3'))
                for h in range(4 if ra_mode > 0 else 0):
                    for is_k in ((0, 1) if ra_mode > 1 else (0,)):
                        banks = []
                        for dkc in range(2):
                            oc = (8 if is_k else 0) + h * 2 + dkc
                            ai = na % 4
                            na += 1
                            A = k.ps[2 + ai]
                            tA = None
                            for kk in range(8):
                                tA = PE.op(nc.tensor.matmul, A[:, 0:nt], lhsT=wbf[:, kk, oc * 128:(oc + 1) * 128], rhs=hs.buf[:, kk, 0:nt],
                                           start=(kk == 0), stop=(kk == 7), wait=[t_ev, t_w] + Afree[ai], sig=(kk == 7))
                            banks.append((A, ai, tA))
                        (A, aA, tA), (B, aB, tB) = banks
                        if not is_k:
                            while kz_pending:
                                kz_pending.pop(0)()
                        co = 2 if is_k else 0
                        cosT = rp.buf[:, co, 0:nt]
                        sinT = rp.buf[:, co + 1, 0:nt]
                        ts_ = [r_.next() for r_ in tt_ring]
                        w1 = DVE.op(nc.vector.tensor_tensor, out=ts_[0].buf[:, 0:nt], in0=A[:, 0:nt], in1=cosT, op=ALU.mult, wait=[tA, t_r] + ts_[0].free)
                        w3 = DVE.op(nc.vector.tensor_tensor, out=ts_[2].buf[:, 0:nt], in0=A[:, 0:nt], in1=sinT, op=ALU.mult, wait=ts_[2].free)
                        Afree[aA] = [w3]
                        w2 = DVE.op(nc.vector.tensor_tensor, out=ts_[1].buf[:, 0:nt], in0=B[:, 0:nt], in1=sinT, op=ALU.mult, wait=[tB] + ts_[1].free)
                        w4 = DVE.op(nc.vector.tensor_tensor, out=ts_[3].buf[:, 0:nt], in0=B[:, 0:nt], in1=cosT, op=ALU.mult, wait=ts_[3].free)
                        Afree[aB] = [w4]
                        if not is_k:
                            oA = o_ring[0].next()
                            oB = o_ring[1].next()
                            pA = POOL.op(nc.gpsimd.tensor_tensor, out=oA.buf[:, 0:nt], in0=ts_[0].buf[:, 0:nt], in1=ts_[1].buf[:, 0:nt], op=ALU.subtract,
                                         wait=[w1, w2] + oA.free)
                            pB = POOL.op(nc.gpsimd.tensor_tensor, out=oB.buf[:, 0:nt], in0=ts_[2].buf[:, 0:nt], in1=ts_[3].buf[:, 0:nt], op=ALU.add,
                                         wait=[w3, w4] + oB.free)
                            for q_ in range(4):
                                ts_[q_].free = [pA, pB]
                            frees = [[], []]
                            for dkc, (osl, po) in enumerate(((oA, pA), (oB, pB))):
                                qs_ = qst_ring.next()
                                c0 = ACT.op(nc.scalar.copy, out=qs_.buf[:, 0, 0:nt], in_=osl.buf[:, 0:nt], wait=[po] + qs_.free)
                                nch = nt // 128
                                c1 = c2 = None
                                for cch in range(nch):
                                    cs0, cs1 = cch * 128, (cch + 1) * 128
                                    c1 = POOL.op(nc.gpsimd.tensor_tensor, out=qs_.buf[:, 1, cs0:cs1], in0=osl.buf[:, cs0:cs1], in1=xi[:, h * 2, :],
                                                 op=ALU.mult, wait=[po] + t_dec + qs_.free)
                                    c2 = DVE.op(nc.vector.tensor_tensor, out=qs_.buf[:, 2, cs0:cs1], in0=osl.buf[:, cs0:cs1], in1=xi[:, h * 2 + 1, :],
                                                op=ALU.mult, wait=[po] + t_dec + qs_.free)
                                osl.free = [c0, c1, c2]
                                td = SP.dma(k.rq[:, h * 2 + dkc, :, off:off + nt].rearrange("v p n -> p v n"), qs_.buf[:, :, 0:nt], qs_.sem,
                                            wait=[c0, c1, c2])
                                qs_.free = [td]
                        else:
                            ks_ = kst_ring.next()
                            oA = o_ring[0].next()
                            oB = o_ring[1].next()
                            pA = POOL.op(nc.gpsimd.tensor_tensor, out=oA.buf[:, 0:nt], in0=ts_[0].buf[:, 0:nt], in1=ts_[1].buf[:, 0:nt], op=ALU.subtract,
                                         wait=[w1, w2] + oA.free)
                            pB = POOL.op(nc.gpsimd.tensor_tensor, out=oB.buf[:, 0:nt], in0=ts_[2].buf[:, 0:nt], in1=ts_[3].buf[:, 0:nt], op=ALU.add,
                                         wait=[w3, w4] + oB.free)
                            for q_ in range(4):
                                ts_[q_].free = [pA, pB]
                            cA = ACT.op(nc.scalar.copy, out=ks_.buf[:, 0, 0:nt], in_=oA.buf[:, 0:nt], wait=[pA] + ks_.free)
                            cB = ACT.op(nc.scalar.copy, out=ks_.buf[:, 1, 0:nt], in_=oB.buf[:, 0:nt], wait=[pB])
                            td = SP.dma(k.rk[h * 2:h * 2 + 2, :, off:off + nt].rearrange("c p n -> p c n"), ks_.buf[:, :, 0:nt], ks_.sem, wait=[cA, cB])
                            ks_.free = [td]
                            if ra_mode == 2:
                                oA.free = [cA]
                                oB.free = [cB]
                                continue
                            def kz_work(h=h, oA=oA, oB=oB, cA=cA, cB=cB, nt4=nt4, off=off, nt=nt):
                                nonlocal Pfree
                                kz = kz_ring.next()
                                tp = e0 = e1 = None
                                for half in range(nt4 // 2):
                                    tts = [2 * half, 2 * half + 1]
                                    for ti, tt in enumerate(tts):
                                        for dkc, osl in enumerate((oA, oB)):
                                            tp = PE.op(nc.tensor.transpose, out=k.ps[6][:, ti * 256 + dkc * 128:ti * 256 + (dkc + 1) * 128],
                                                       in_=osl.buf[:, tt * 128:(tt + 1) * 128], identity=k.ident,
                                                       wait=[cA, cB] + Pfree + k.t_init, sig=(ti == 1 and dkc == 1))
                                    for ti, tt in enumerate(tts):
                                        e0 = ACT.op(nc.scalar.activation, out=kz.buf[:, tt, 0, :], in_=k.ps[6][:, ti * 256:(ti + 1) * 256], func=AF.Identity,
                                                    scale=zc[:, h:h + 1], wait=[tp] + t_dec + kz.free)
                                        e1 = ACT.op(nc.scalar.activation, out=kz.buf[:, tt, 1, :], in_=k.ps[6][:, ti * 256:(ti + 1) * 256], func=AF.Identity,
                                                    scale=zc[:, 4 + h:5 + h], wait=[tp] + t_dec + kz.free)
                                    Pfree = [e0, e1]
                                oA.free = [cA, e0, e1]
                                oB.free = [cB, e0, e1]
                                tz = None
                                if ra_mode == 4:
                                    kz.free = [e0, e1]
                                    return
                                for d_ in range(2):
                                    tz = SP.dma(k.rkz[d_][off:off + nt, h * 256:(h + 1) * 256].rearrange("(t p) d -> p t d", p=128),
                                                kz.buf[:, 0:nt4, d_, :], kz.sem, wait=[e0, e1])
                                kz.free = [tz]
                            kz_pending.append(kz_work)
                while kz_pending:
                    kz_pending.pop(0)()
                hs.free = [(PE.s, PE.s.cnt), t_hst]
                rp.free = [(DVE.s, DVE.s.cnt)]
        barrier(k)
        if k.ret_stop == 1:
            return
        with Phase(k, f"rb{i}") as ph:
            wbf = ph.sb([128, 8, 4096], BF16, "wbf")
            wsem = k.borrow_sw()
            t_w = None
            for kk in range(8):
                t_w = POOL.dma(wbf[:, kk, :], k.r_w_in[j][kk * 128:(kk + 1) * 128, 2048:6144], wsem)
            hT_ring = ph.ring(2, [128, 8, 512], BF16, dma=True, name="hT")
            vst_ring = ph.ring(2, [128, 2048], BF16, dma=True, name="vst")
            gst_ring = ph.ring(2, [128, 2048], BF16, dma=True, name="gst")
            SBm = ph.sb([128, 4, 2, 512], F32, "SBm")
            bl_ring = ph.ring(3, [128, 3072], BF16, dma=True, name="bl")
            sbst_ring = ph.ring(2, [128, 8, 512], BF16, dma=True, name="sbst")
            t_z2 = DVE.op(nc.vector.memset, SBm[:], 0.0)
            st_b = {h: [t_z2] for h in range(4)}
            Dfree = [[], []]
            loads = {}
            border = [8, 7, 6, 5, 4, 3, 2, 1, 0]

            def issue_loadb(bi):
                if bi >= len(border):
                    return
                b = border[bi]
                src, off, nt, r = blocks[b]
                hs = hT_ring.next()
                t = SP.dma(hs.buf[:, :, 0:nt], k.hTd[b][:, :, 0:nt], hs.sem, wait=hs.free)
                loads[bi] = (hs, t)

            tiles = []
            for bi, b in enumerate(border):
                src, off, nt, r = blocks[b]
                for tt in reversed(range(nt // 128)):
                    tiles.append((bi, b, tt, off // 128 + tt))
            vtok = {}
            bw = {}
            LAGB = 2

            def bw_load(c):
                sl = bl_ring.next()
                SP.dma(sl.buf[:, 0:1024], k.rkz[1][c * 128:(c + 1) * 128, :], sl.sem, wait=sl.free)
                t = SP.dma(sl.buf[:, 1024:3072], k.rv[c * 128:(c + 1) * 128, :], sl.sem, wait=[vtok[c]])
                bw[c] = dict(sl=sl, t_l=t)

            def bw_begin(c):
                d = bw[c]
                ss = sbst_ring.next()
                d["ss"] = ss
                d["tcs"] = [ACT.op(nc.scalar.copy, out=ss.buf[:, h * 2:h * 2 + 2, :], in_=SBm[:, h, :, :], wait=st_b[h] + ss.free)
                            for h in range(4)]

            def bw_head(c, h):
                d = bw[c]
                sl = d["sl"]
                tds = []
                for dkc in range(2):
                    Db = k.ps[6 + dkc]
                    tdm = PE.op(nc.tensor.matmul, Db[:, :], lhsT=sl.buf[:, h * 256 + dkc * 128:h * 256 + (dkc + 1) * 128],
                                rhs=sl.buf[:, 1024 + h * 512:1024 + (h + 1) * 512], start=True, stop=True, wait=[d["t_l"]] + Dfree[dkc])
                    tu = DVE.op(nc.vector.scalar_tensor_tensor, out=SBm[:, h, dkc, :], in0=SBm[:, h, dkc, :], scalar=zc[:, 12 + h:13 + h],
                                in1=Db[:, :], op0=ALU.mult, op1=ALU.add, wait=[tdm, d["tcs"][h]] + t_dec + st_b[h])
                    Dfree[dkc] = [tu]
                    tds.append(tu)
                st_b[h] = tds

            def bw_end(c):
                d = bw.pop(c)
                d["sl"].free = [(PE.s, PE.s.cnt)]
                d["ss"].free = [SP.dma(k.rsb[c][:, :, :], d["ss"].buf[:], d["ss"].sem, wait=[d["tcs"][-1]])]

            issue_loadb(0)
            Yfree = [[], [], [], []]
            yc = 0
            cur_bi = -1
            hs = t_h = None
            for ti, (bi, b, tt, c) in enumerate(tiles):
                if bi != cur_bi:
                    if hs is not None:
                        hs.free = [(PE.s, PE.s.cnt)]
                    issue_loadb(bi + 1)
                    hs, t_h = loads.pop(bi)
                    cur_bi = bi
                src, off, nt, r = blocks[b]
                cb = tiles[ti - LAGB][3] if ti >= LAGB else None
                if ti >= 1:
                    bw_load(tiles[ti - 1][3])
                if cb is not None:
                    bw_begin(cb)
                vs_ = vst_ring.next()
                gs_ = gst_ring.next()
                tv = tg = None
                for grp in range(8):
                    yi = yc % 4
                    yc += 1
                    Y = k.ps[yi]
                    tY = None
                    for kk in range(8):
                        tY = PE.op(nc.tensor.matmul, Y[:, :], lhsT=hs.buf[:, kk, tt * 128:(tt + 1) * 128], rhs=wbf[:, kk, grp * 512:(grp + 1) * 512],
                                   start=(kk == 0), stop=(kk == 7), wait=[t_h, t_w] + Yfree[yi], sig=(kk == 7))
                    if grp < 4:
                        tv = DVE.op(nc.vector.tensor_copy, out=vs_.buf[:, grp * 512:(grp + 1) * 512], in_=Y[:, :], wait=[tY] + vs_.free)
                        Yfree[yi] = [tv]
                    else:
                        g4 = grp - 4
                        tg = ACT.op(nc.scalar.activation, out=gs_.buf[:, g4 * 512:(g4 + 1) * 512], in_=Y[:, :], func=AF.Silu, wait=[tY] + gs_.free)
                        Yfree[yi] = [tg]
                    if cb is not None and grp % 2 == 1:
                        bw_head(cb, grp // 2)
                r0 = off + tt * 128
                vtok[c] = SP.dma(k.rv[r0:r0 + 128, :], vs_.buf[:], vs_.sem, wait=[tv])
                vs_.free = [vtok[c]]
                gs_.free = [SP.dma(k.rg[r0:r0 + 128, :], gs_.buf[:], gs_.sem, wait=[tg])]
                if cb is not None:
                    bw_end(cb)
            hs.free = [(PE.s, PE.s.cnt)]
            for ti in range(len(tiles), len(tiles) + LAGB):
                if ti - 1 < len(tiles):
                    bw_load(tiles[ti - 1][3])
                cb = tiles[ti - LAGB][3]
                bw_begin(cb)
                for h in range(4):
                    bw_head(cb, h)
                bw_end(cb)
        barrier(k)
        if k.ret_stop == 2:
            return
        with Phase(k, f"rc{i}") as ph:
            SFm = ph.sb([128, 4, 2, 512], F32, "SFm")
            SFb2 = ph.sb([128, 2, 4, 2, 512], BF16, "SFb")
            sfb_free = {}
            t_z1 = DVE.op(nc.vector.memset, SFm[:], 0.0)
            st_f = {h: [t_z1] for h in range(4)}
            sfb_free = {}
            Ofree = [[], [], [], []]
            Dfree = [[], []]
            stt_ = {"ST": [], "PB": []}
            eps_gn = k.misc[:, 6:7]
            r2_mode = int(_os.environ.get('R2_MODE', '3'))

            def run_seq(chunks, need_out):
                n = len(chunks)
                with Phase(k, f"rcf{i}_{chunks[0]}") as pf:
                    fq_ring = pf.ring(3, [128, 3, 8, 128], BF16, dma=True, name="fq")
                    fk_ring = pf.ring(2, [128, 8, 128], BF16, dma="sw", name="fk")
                    fl_ring = pf.ring(3, [128, 5120], BF16, dma=True, name="fl")
                    fs_ring = pf.ring(3, [128, 8, 512], BF16, dma="sw", name="fs")
                    at_ring = pf.ring(8, [128, 128], BF16, name="at")
                    on_ring = pf.ring(4, [128, 512], F32, name="on")
                    og_ring = pf.ring(8, [128, 512], F32, name="og")
                    st_ring = pf.ring(8, [128, 16], F32, name="st")
                    ot_ring = pf.ring(2, [128, 16, 256], BF16, dma=True, name="ot")
                    fl = {}
                    casts = {}
                    ats = {}
                    outs = {}

                    def load_f(idx):
                        if idx >= n:
                            return
                        c = chunks[idx]
                        tk0, tk1 = c * 128, (c + 1) * 128
                        s1 = fl_ring.next()
                        SP.dma(s1.buf[:, 0:1024], k.rkz[0][tk0:tk1, :], s1.sem, wait=s1.free)
                        SP.dma(s1.buf[:, 1024:3072], k.rv[tk0:tk1, :], s1.sem)
                        t1 = SP.dma(s1.buf[:, 3072:5120], k.rg[tk0:tk1, :], s1.sem)
                        s2 = t2 = s3 = t3 = s4 = t4 = None
                        if need_out:
                            s2 = fq_ring.next()
                            for v_ in range(3):
                                t2 = ACT.dma(s2.buf[:, v_, :, :], k.rq[v_][:, :, tk0:tk1].rearrange("c p n -> p c n"), s2.sem,
                                             wait=s2.free if v_ == 0 else [])
                            s3 = fk_ring.next()
                            t3 = POOL.dma(s3.buf[:], k.rk[:, :, tk0:tk1].rearrange("c p n -> p c n"), s3.sem, wait=s3.free)
                            s4 = fs_ring.next()
                            t4 = POOL.dma(s4.buf[:], k.rsb[c][:, :, :], s4.sem, wait=s4.free)
                        fl[idx] = (s1, t1, s2, t2, s3, t3, s4, t4)

                    def casts_A(idx):
                        par = idx % 2
                        tcps = []
                        for h in range(4):
                            tcps.append(ACT.op(nc.scalar.copy, out=SFb2[:, par, h, :, :], in_=SFm[:, h, :, :],
                                               wait=st_f[h] + sfb_free.get((par, h), [])))
                        casts[idx] = tcps

                    def dS_A(idx, h):
                        s1, t1, s2, t2, s3, t3, s4, t4 = fl[idx]
                        tcps = casts[idx]
                        Vh = s1.buf[:, 1024 + h * 512:1024 + (h + 1) * 512]
                        tds = []
                        for dkc in range(2):
                            Db = k.ps[6 + dkc]
                            tdm = PE.op(nc.tensor.matmul, Db[:, :], lhsT=s1.buf[:, h * 256 + dkc * 128:h * 256 + (dkc + 1) * 128], rhs=Vh,
                                        start=True, stop=True, wait=[t1] + Dfree[dkc])
                            tu = DVE.op(nc.vector.scalar_tensor_tensor, out=SFm[:, h, dkc, :], in0=SFm[:, h, dkc, :], scalar=zc[:, 8 + h:9 + h],
                                        in1=Db[:, :], op0=ALU.mult, op1=ALU.add, wait=[tdm, tcps[h]] + t_dec + st_f[h])
                            Dfree[dkc] = [tu]
                            tds.append(tu)
                        st_f[h] = tds
                        if not need_out and h == 3:
                            s1.free = [(PE.s, PE.s.cnt)]

                    def stage_B(idx):
                        s1, t1, s2, t2, s3, t3, s4, t4 = fl[idx]
                        tS = None
                        for h in range(4):
                            for dkc in range(2):
                                tS = PE.op(nc.tensor.matmul, k.ps[0][:, h * 128:(h + 1) * 128], lhsT=s3.buf[:, h * 2 + dkc, :],
                                           rhs=s2.buf[:, 0, h * 2 + dkc, :], start=(dkc == 0), stop=(dkc == 1), wait=[t2, t3] + stt_["ST"],
                                           sig=(h == 3 and dkc == 1))
                        s3.free = [tS]
                        lst = []
                        for h in range(4):
                            at = at_ring.next()
                            tm = DVE.op(nc.vector.tensor_tensor, out=at.buf[:], in0=k.ps[0][:, h * 128:(h + 1) * 128], in1=DT[:, h, :], op=ALU.mult,
                                        wait=[tS] + t_dec + at.free)
                            lst.append((at, tm))
                        stt_["ST"] = [lst[-1][1]]
                        ats[idx] = lst

                    tOd = {}

                    def O_C(idx, h):
                        s1, t1, s2, t2, s3, t3, s4, t4 = fl[idx]
                        par = idx % 2
                        tcps = casts[idx]
                        at, tm = ats[idx][h]
                        Ob = k.ps[1 + h]
                        Vh = s1.buf[:, 1024 + h * 512:1024 + (h + 1) * 512]
                        PE.op(nc.tensor.matmul, Ob[:, :], lhsT=at.buf[:], rhs=Vh, start=True, stop=False, wait=[tm, t1] + Ofree[h], sig=False)
                        for dkc in range(2):
                            PE.op(nc.tensor.matmul, Ob[:, :], lhsT=s2.buf[:, 1, h * 2 + dkc, :], rhs=SFb2[:, par, h, dkc, :], start=False, stop=False,
                                  wait=[tcps[h], t2], sig=False)
                        tO = None
                        for dkc in range(2):
                            tO = PE.op(nc.tensor.matmul, Ob[:, :], lhsT=s2.buf[:, 2, h * 2 + dkc, :], rhs=s4.buf[:, h * 2 + dkc, :], start=False,
                                       stop=(dkc == 1), wait=[t4], sig=(dkc == 1))
                        at.free = [tO]
                        sfb_free[(par, h)] = [tO]
                        tOd.setdefault(idx, []).append(tO)

                    def chain_C(idx):
                        s1, t1, s2, t2, s3, t3, s4, t4 = fl[idx]
                        tOs = tOd.pop(idx)
                        casts.pop(idx)
                        ats.pop(idx)
                        s2.free = [tOs[-1]]
                        s4.free = [tOs[-1]]
                        stsl = [st_ring.next() for _ in range(4)]
                        b1 = [DVE.op(nc.vector.bn_stats, out=stsl[h].buf[:, 0:6], in_=k.ps[1 + h][:, :], wait=[tOs[h]] + stsl[h].free) for h in range(4)]
                        b2 = [DVE.op(nc.vector.bn_aggr, out=stsl[h].buf[:, 6:8], in_=stsl[h].buf[:, 0:6], wait=[b1[h]]) for h in range(4)]
                        b3 = [ACT.op(nc.scalar.activation, out=stsl[h].buf[:, 8:9], in_=stsl[h].buf[:, 7:8], func=AF.Sqrt, bias=eps_gn, wait=[b2[h]])
                              for h in range(4)]
                        b4 = [DVE.op(nc.vector.reciprocal, out=stsl[h].buf[:, 9:10], in_=stsl[h].buf[:, 8:9], wait=[b3[h]]) for h in range(4)]
                        b5 = [DVE.op(nc.vector.tensor_scalar, out=stsl[h].buf[:, 10:11], in0=stsl[h].buf[:, 6:7], scalar1=stsl[h].buf[:, 9:10],
                                     scalar2=-1.0, op0=ALU.mult, op1=ALU.mult, wait=[b4[h]]) for h in range(4)]
                        ons = [on_ring.next() for _ in range(4)]
                        b6 = []
                        for h in range(4):
                            t_ = ACT.op(nc.scalar.activation, out=ons[h].buf[:], in_=k.ps[1 + h][:, :], func=AF.Identity, scale=stsl[h].buf[:, 9:10],
                                        bias=stsl[h].buf[:, 10:11], wait=[b5[h]] + ons[h].free)
                            b6.append(t_)
                            Ofree[h] = [t_]
                            stsl[h].free = [t_]
                        ogs = []
                        for h in range(4):
                            og = og_ring.next()
                            b7 = POOL.op(nc.gpsimd.tensor_tensor, out=og.buf[:], in0=ons[h].buf[:], in1=s1.buf[:, 3072 + h * 512:3072 + (h + 1) * 512],
                                         op=ALU.mult, wait=[b6[h], t1] + og.free)
                            ons[h].free = [b7]
                            ogs.append((og, b7))
                        s1.free = [tOs[-1], ogs[-1][1]]
                        outs[idx] = ogs

                    otst = {}

                    def T_D(idx, h):
                        if idx % 2 == 0 and h == 0:
                            otst["slot"] = ot_ring.next()
                            otst["first"] = True
                        ot = otst["slot"]
                        col = (idx % 2) * 128
                        og, b7 = outs[idx][h]
                        tp = None
                        for wc in range(4):
                            tp = PE.op(nc.tensor.transpose, out=k.ps[5][:, wc * 128:(wc + 1) * 128], in_=og.buf[:, wc * 128:(wc + 1) * 128],
                                       identity=k.ident, wait=[b7] + stt_["PB"] + k.t_init, sig=(wc == 3))
                        og.free = [tp]
                        tev = ACT.op(nc.scalar.copy, out=ot.buf[:, h * 4:(h + 1) * 4, col:col + 128],
                                     in_=k.ps[5][:, :].rearrange("p (w n) -> p w n", n=128), wait=[tp] + (ot.free if otst["first"] else []))
                        otst["first"] = False
                        stt_["PB"] = [tev]
                        otst["tev"] = tev

                    def store_D(idx):
                        outs.pop(idx)
                        if not (idx % 2 == 1 or idx == n - 1):
                            return
                        ot = otst["slot"]
                        i0_ = idx - (idx % 2)
                        ntk = (idx - i0_ + 1) * 128
                        tk0 = chunks[i0_] * 128
                        tst = None
                        for h in range(4):
                            tst = SP.dma(k.ogT[h * 4:(h + 1) * 4, :, tk0:tk0 + ntk].rearrange("w p n -> p w n"), ot.buf[:, h * 4:(h + 1) * 4, 0:ntk],
                                         ot.sem, wait=[otst["tev"]])
                        ot.free = [tst]

                    load_f(0)
                    load_f(1)
                    for it in range(n + 2):
                        do_A = it < n
                        do_C = need_out and 0 <= it - 1 < n
                        do_D = need_out and 0 <= it - 2 < n
                        if do_A:
                            casts_A(it)
                            if need_out:
                                stage_B(it)
                        for h in range(4):
                            if do_A:
                                dS_A(it, h)
                            if do_C:
                                O_C(it - 1, h)
                            if do_D:
                                T_D(it - 2, h)
                        if do_C:
                            chain_C(it - 1)
                        if do_D:
                            store_D(it - 2)
                        load_f(it + 2)
                barrier(k)

            if need_ctx:
                run_seq([32, 33] + list(range(32)), True)
            else:
                run_seq([32, 33], False)
                run_seq(list(range(32)), True)
    barrier(k)


def _consts():
    p = np.arange(128)
    ident = np.eye(128, dtype=np.float32)
    rot = np.zeros((128, 128), np.float32)
    for m in range(128):
        if (m % 64) < 32:
            rot[m + 32, m] = -1.0
        else:
            rot[m - 32, m] = 1.0
    onesdiv = np.full((128, 128), 1.0 / 128, np.float32)
    diff = p[None, :] - p[:, None]
    RP = np.maximum(diff, 0).astype(np.float32)
    RN = np.maximum(-diff, 0).astype(np.float32)
    IDX1 = np.tile((p + 1)[None, :], (128, 1)).astype(np.float32)
    IDXB = np.tile((128 - p)[None, :], (128, 1)).astype(np.float32)
    misc = np.zeros((128, 128), np.float32)
    misc[:, 0] = 127 - p
    misc[:, 1] = p
    misc[:, 2] = 128.0
    misc[:, 3] = 1.0
    misc[:, 4] = QK_EPS
    misc[:, 5] = LN_EPS
    misc[:, 6] = GN_EPS
    cst = np.concatenate([ident, rot, onesdiv, RP, RN, IDX1, IDXB, misc], axis=1)
    t = np.arange(S)
    row = (t // GRID_W).astype(np.float64)
    col = (t % GRID_W).astype(np.float64)
    fr = THETA ** (-(np.arange(32, dtype=np.float64)) / 32)
    angA = np.zeros((128, S))
    for pp in range(128):
        angA[pp] = (row if pp < 64 else col) * fr[pp % 32]
    ropeA = np.stack([np.cos(angA), np.sin(angA)]).astype(np.float32)
    pos = np.concatenate([L + np.arange(S), np.arange(L)]).astype(np.float64)
    frR = THETA ** (-(np.arange(128, dtype=np.float64)) / 128)
    angR = frR[:, None] * pos[None, :]
    ropeR = np.stack([np.cos(angR), np.sin(angR), np.cos(angR) / 16.0, np.sin(angR) / 16.0]).astype(np.float32)
    return cst, ropeA, ropeR


_CACHE = {}


def _host_inputs(inputs):
    f = lambda a: np.ascontiguousarray(np.asarray(a, dtype=np.float32))
    cst, ropeA, ropeR = _consts()
    qk = np.stack([inputs["attn_q_scale"][0], inputs["attn_k_scale"][0], inputs["attn_q_scale"][1], inputs["attn_k_scale"][1]], axis=1)
    gn = np.asarray(inputs["ret_gn_g"]).reshape(2, 16, 128).transpose(2, 0, 1).reshape(128, 32)
    lg = np.stack([np.asarray(inputs["ret_log_decay_fwd"]), np.asarray(inputs["ret_log_decay_bwd"])], axis=1).reshape(1, 16)
    lg = np.tile(lg, (128, 1))
    shared = {
        "mod_w": f(inputs["mod_w"]), "mod_b": f(inputs["mod_b"]), "ln_g": f(inputs["ln_g"]), "ln_b": f(inputs["ln_b"]),
        "attn_w_in": f(inputs["attn_w_in"]), "attn_w_out": f(inputs["attn_w_out"]), "attn_qk": f(qk),
        "ret_w_in": f(inputs["ret_w_in"]), "ret_w_out": f(inputs["ret_w_out"]), "ret_gn": f(gn), "ret_lg": f(lg),
        "cst": f(cst), "ropeA": f(ropeA), "ropeR": f(ropeR),
    }
    maps = []
    cc = np.asarray(inputs["c_ctx"], np.float32).reshape(8, 128).T
    for b in range(8):
        cb = np.asarray(inputs["c"][b], np.float32).reshape(8, 128).T
        cvec = np.stack([cb, cc], axis=2).reshape(128, 16)
        m = dict(shared)
        m["x"] = f(inputs["x"][b])
        m["ctx"] = f(inputs["ctx"][b])
        m["cvec"] = f(cvec)
        maps.append(m)
    return maps


def kernel(**inputs):
    if "nc" not in _CACHE:
        _CACHE["nc"] = build_program()
    nc = _CACHE["nc"]
    maps = _host_inputs(inputs)
    res = run_bass_kernel_spmd(nc, maps, core_ids=list(range(8)))
    return np.stack([np.asarray(r["out"], dtype=np.float32) for r in res.results], axis=0)
```
